# Optimizing a Trainium2 kernel written in Bass

```python
import math
import jax, jax.numpy as jnp
from jax import lax
import numpy as np

D_MODEL = 1024
BATCH = 8
SEQ = 2048
DEPTH = 4

CHUNK = 64
Q_BLOCK = 128
N_MEM = 256
EPS = 1e-6
NEG_INF = -1e30
ROPE_THETA = 500000.0
RET_THETA = 10000.0

RET_HEADS = 4
RET_DK = 64
RET_DV = 128
MLA_HEADS = 8
MLA_Q_RANK = 256
MLA_KV_RANK = 128
MLA_NOPE = 64
MLA_ROPE = 32
MLA_DV = 64
DIFF_HEADS = 4
DIFF_HD = 64
DIFF_ROT = DIFF_HD // 4
N_BRANCH = 3
BRANCH_WIDTH = 512
CROSS_HEADS = 4
CROSS_HD = D_MODEL // CROSS_HEADS
D_FF = 2816

IN_SPLITS = (
    RET_HEADS * RET_DK,
    RET_HEADS * RET_DK,
    RET_HEADS * RET_DV,
    RET_HEADS * RET_DV,
    MLA_Q_RANK,
    MLA_KV_RANK + MLA_ROPE,
    2 * DIFF_HEADS * DIFF_HD,
    2 * DIFF_HEADS * DIFF_HD,
    2 * DIFF_HEADS * DIFF_HD,
    N_BRANCH * D_MODEL,
)
IN_WIDTH = sum(IN_SPLITS)

kernel_name = 'hybrid_streaming_encoder_block'


def rms_norm(x, gain=None):
    xf = x.astype(jnp.float32)
    y = xf * lax.rsqrt(jnp.mean(xf * xf, axis=-1, keepdims=True) + EPS)
    if gain is not None:
        y = y * gain.astype(jnp.float32)
    return y.astype(x.dtype)


def rope_angles(positions, dim, theta):
    inv = 1.0 / (theta ** (jnp.arange(0, dim, 2, dtype=jnp.float32) / dim))
    ang = positions.astype(jnp.float32)[..., None] * inv
    return jnp.cos(ang), jnp.sin(ang)


def apply_rope(x, cos, sin):
    half = x.shape[-1] // 2
    x1, x2 = x[..., :half], x[..., half:]
    c = cos.astype(x.dtype)
    s = sin.astype(x.dtype)
    return jnp.concatenate([x1 * c - x2 * s, x2 * c + x1 * s], axis=-1)


def partial_rope(x, cos, sin, rot):
    return jnp.concatenate([apply_rope(x[..., :rot], cos, sin), x[..., rot:]], axis=-1)


def to_blocks(t):
    b, s = t.shape[:2]
    return jnp.moveaxis(t.reshape((b, s // Q_BLOCK, Q_BLOCK) + t.shape[2:]), 1, 0)


def from_blocks(t):
    t = jnp.moveaxis(t, 0, 1)
    return t.reshape((t.shape[0], t.shape[1] * t.shape[2]) + t.shape[3:])


def chunk_causal_mask(block_idx, seq):
    q_pos = block_idx * Q_BLOCK + jnp.arange(Q_BLOCK)
    k_pos = jnp.arange(seq)
    return (k_pos[None, :] // CHUNK) <= (q_pos[:, None] // CHUNK)


def masked_softmax(scores, mask):
    s = jnp.where(mask, scores.astype(jnp.float32), NEG_INF)
    return jax.nn.softmax(s, axis=-1)


def swiglu(h, w13, w2):
    gate, up = jnp.split(h @ w13, 2, axis=-1)
    return (jax.nn.silu(gate) * up) @ w2


def retention(q, k, v, g):
    b, s, h, dk = q.shape
    nc = s // CHUNK
    dt = q.dtype
    q = q * (dk ** -0.5)
    log_gamma = jnp.log(1.0 - 2.0 ** (-5.0 - jnp.arange(h, dtype=jnp.float32)))
    idx = jnp.arange(CHUNK, dtype=jnp.float32)
    intra_decay = jnp.exp(log_gamma[:, None, None] * jnp.abs(idx[:, None] - idx[None, :]))
    q_decay = jnp.exp(log_gamma[None, :] * (idx[:, None] + 1.0))
    k_decay = jnp.exp(log_gamma[None, :] * (CHUNK - 1.0 - idx[:, None]))
    chunk_decay = jnp.exp(log_gamma * CHUNK).astype(dt)
    qc = q.reshape(b, nc, CHUNK, h, dk)
    kc = k.reshape(b, nc, CHUNK, h, dk)
    vc = v.reshape(b, nc, CHUNK, h, -1)
    scores = jnp.einsum('bcnhd,bcmhd->bchnm', qc, kc) * intra_decay.astype(dt)
    o_intra = jnp.einsum('bchnm,bcmhe->bcnhe', scores, vc)
    kv = jnp.einsum('bcmhd,bcmhe->bchde', kc * k_decay.astype(dt)[:, :, None], vc)

    def step(state, kv_c):
        return state * chunk_decay[None, :, None, None] + kv_c, state

    _, s_prev = lax.scan(step, jnp.zeros_like(kv[:, 0]), jnp.moveaxis(kv, 1, 0))
    s_prev = jnp.moveaxis(s_prev, 0, 1)
    o_cross = jnp.einsum('bcnhd,bchde->bcnhe', qc * q_decay.astype(dt)[:, :, None], s_prev)
    o = rms_norm(o_intra + o_cross).reshape(b, s, -1)
    return jax.nn.silu(g) * o


def latent_attention(c_q_raw, kv_a, q_norm, kv_norm, wq_b, wkv_b, cos, sin):
    b, s, _ = c_q_raw.shape
    q = (rms_norm(c_q_raw, q_norm) @ wq_b).reshape(b, s, MLA_HEADS, MLA_NOPE + MLA_ROPE)
    q_nope = q[..., :MLA_NOPE]
    q_rope = apply_rope(q[..., MLA_NOPE:], cos[:, :, None], sin[:, :, None])
    c_kv, k_rope = kv_a[..., :MLA_KV_RANK], kv_a[..., MLA_KV_RANK:]
    k_rope = apply_rope(k_rope, cos, sin)
    kv = (rms_norm(c_kv, kv_norm) @ wkv_b).reshape(b, s, MLA_HEADS, MLA_NOPE + MLA_DV)
    k_nope, v = kv[..., :MLA_NOPE], kv[..., MLA_NOPE:]
    scale = (MLA_NOPE + MLA_ROPE) ** -0.5

    def block(args):
        qn, qr, i = args
        sc = (jnp.einsum('bqhd,bkhd->bhqk', qn, k_nope)
              + jnp.einsum('bqhd,bkd->bhqk', qr, k_rope)) * scale
        p = masked_softmax(sc, chunk_causal_mask(i, s))
        return jnp.einsum('bhqk,bkhe->bqhe', p.astype(v.dtype), v)

    o = lax.map(block, (to_blocks(q_nope), to_blocks(q_rope), jnp.arange(s // Q_BLOCK)))
    return from_blocks(o).reshape(b, s, -1)


def diff_attention(q, k, v, lam_params, lambda_init, cos, sin):
    b, s, _ = q.shape
    q = q.reshape(b, s, DIFF_HEADS, 2, DIFF_HD)
    k = k.reshape(b, s, DIFF_HEADS, 2, DIFF_HD)
    v = v.reshape(b, s, DIFF_HEADS, 2 * DIFF_HD)
    c, sn = cos[:, :, None, None], sin[:, :, None, None]
    q = partial_rope(q, c, sn, DIFF_ROT)
    k = partial_rope(k, c, sn, DIFF_ROT)
    lp = lam_params.astype(jnp.float32)
    lam = jnp.exp(jnp.sum(lp[0] * lp[1])) - jnp.exp(jnp.sum(lp[2] * lp[3])) + lambda_init
    scale = DIFF_HD ** -0.5

    def block(args):
        qb, i = args
        sc = jnp.einsum('bqhjd,bkhjd->bhjqk', qb, k) * scale
        p = masked_softmax(sc, chunk_causal_mask(i, s))
        w = p[:, :, 0] - lam * p[:, :, 1]
        return jnp.einsum('bhqk,bkhe->bqhe', w.astype(v.dtype), v)

    o = from_blocks(lax.map(block, (to_blocks(q), jnp.arange(s // Q_BLOCK))))
    o = rms_norm(o) * (1.0 - lambda_init)
    return o.reshape(b, s, -1)


def cross_attention(h, mem_n, wq, wkv, wo):
    b, s, _ = h.shape
    q = (h @ wq).reshape(b, s, CROSS_HEADS, CROSS_HD)
    kv = (mem_n @ wkv).reshape(b, mem_n.shape[1], 2, CROSS_HEADS, CROSS_HD)
    k, v = kv[:, :, 0], kv[:, :, 1]
    sc = jnp.einsum('bqhd,bkhd->bhqk', q, k) * (CROSS_HD ** -0.5)
    p = jax.nn.softmax(sc.astype(jnp.float32), axis=-1).astype(v.dtype)
    o = jnp.einsum('bhqk,bkhe->bqhe', p, v).reshape(b, s, D_MODEL)
    return o @ wo


def token_mixing(h, w_in, mla_q_norm, mla_kv_norm, mla_wq_b, mla_wkv_b, diff_lambda,
                 w_branch, w_out, lambda_init, ret_rope, mla_rope, diff_rope):
    b, s, _ = h.shape
    split_points = [int(i) for i in np.cumsum(IN_SPLITS)[:-1]]
    (ret_q, ret_k, ret_v, ret_g, mla_qa, mla_kva,
     d_q, d_k, d_v, gate_logits) = jnp.split(h @ w_in, split_points, axis=-1)
    rc, rs = ret_rope[0][:, :, None], ret_rope[1][:, :, None]
    y_ret = retention(apply_rope(ret_q.reshape(b, s, RET_HEADS, RET_DK), rc, rs),
                      apply_rope(ret_k.reshape(b, s, RET_HEADS, RET_DK), rc, rs),
                      ret_v.reshape(b, s, RET_HEADS, RET_DV), ret_g)
    y_mla = latent_attention(mla_qa, mla_kva, mla_q_norm, mla_kv_norm, mla_wq_b, mla_wkv_b,
                             mla_rope[0], mla_rope[1])
    y_diff = diff_attention(d_q, d_k, d_v, diff_lambda, lambda_init, diff_rope[0], diff_rope[1])
    branches = jnp.stack([y_ret, y_mla, y_diff], axis=2)
    proj = jnp.einsum('bsnw,nwd->bsnd', branches, w_branch)
    gates = jax.nn.sigmoid(gate_logits.reshape(b, s, N_BRANCH, D_MODEL))
    return jnp.sum(gates * proj, axis=2) @ w_out


def setup_inputs(seed: int = 0) -> dict:
    key = jax.random.key(seed)
    ks = jax.random.split(key, 24)
    f32 = jnp.float32

    def dense(k, shape, fan_in):
        return jax.random.normal(k, shape, f32) * (fan_in ** -0.5)

    def gains(k, shape):
        return 1.0 + 0.05 * jax.random.normal(k, shape, f32)

    offsets = jax.random.randint(ks[2], (BATCH, 1), 0, 64) * CHUNK
    positions = (offsets + jnp.arange(SEQ)[None, :]).astype(jnp.int32)
    return {
        'x': jax.random.normal(ks[0], (BATCH, SEQ, D_MODEL), f32),
        'mem': jax.random.normal(ks[1], (BATCH, N_MEM, D_MODEL), f32),
        'positions': positions,
        'ffn1_norms': gains(ks[3], (DEPTH, 2, D_MODEL)),
        'ffn1_w13': dense(ks[4], (DEPTH, D_MODEL, 2 * D_FF), D_MODEL),
        'ffn1_w2': dense(ks[5], (DEPTH, D_FF, D_MODEL), D_FF),
        'mix_norms': gains(ks[6], (DEPTH, 2, D_MODEL)),
        'w_in': dense(ks[7], (DEPTH, D_MODEL, IN_WIDTH), D_MODEL),
        'mla_q_norm': gains(ks[8], (DEPTH, MLA_Q_RANK)),
        'mla_kv_norm': gains(ks[9], (DEPTH, MLA_KV_RANK)),
        'mla_wq_b': dense(ks[10], (DEPTH, MLA_Q_RANK, MLA_HEADS * (MLA_NOPE + MLA_ROPE)), MLA_Q_RANK),
        'mla_wkv_b': dense(ks[11], (DEPTH, MLA_KV_RANK, MLA_HEADS * (MLA_NOPE + MLA_DV)), MLA_KV_RANK),
        'diff_lambda': 0.1 * jax.random.normal(ks[12], (DEPTH, 4, DIFF_HD), f32),
        'w_branch': dense(ks[13], (DEPTH, N_BRANCH, BRANCH_WIDTH, D_MODEL), BRANCH_WIDTH),
        'w_out': dense(ks[14], (DEPTH, D_MODEL, D_MODEL), D_MODEL),
        'cross_norms': gains(ks[15], (DEPTH, 3, D_MODEL)),
        'cross_wq': dense(ks[16], (DEPTH, D_MODEL, D_MODEL), D_MODEL),
        'cross_wkv': dense(ks[17], (DEPTH, D_MODEL, 2 * D_MODEL), D_MODEL),
        'cross_wo': dense(ks[18], (DEPTH, D_MODEL, D_MODEL), D_MODEL),
        'ffn2_norms': gains(ks[19], (DEPTH, 2, D_MODEL)),
        'ffn2_w13': dense(ks[20], (DEPTH, D_MODEL, 2 * D_FF), D_MODEL),
        'ffn2_w2': dense(ks[21], (DEPTH, D_FF, D_MODEL), D_FF),
    }


def reference(x, mem, positions, ffn1_norms, ffn1_w13, ffn1_w2, mix_norms, w_in,
              mla_q_norm, mla_kv_norm, mla_wq_b, mla_wkv_b, diff_lambda, w_branch, w_out,
              cross_norms, cross_wq, cross_wkv, cross_wo, ffn2_norms, ffn2_w13, ffn2_w2):
    ret_rope = rope_angles(positions, RET_DK, RET_THETA)
    mla_rope = rope_angles(positions, MLA_ROPE, ROPE_THETA)
    diff_rope = rope_angles(positions, DIFF_ROT, ROPE_THETA)
    for l in range(DEPTH):
        lambda_init = 0.8 - 0.6 * math.exp(-0.3 * l)
        h = rms_norm(x, ffn1_norms[l, 0])
        x = x + 0.5 * rms_norm(swiglu(h, ffn1_w13[l], ffn1_w2[l]), ffn1_norms[l, 1])
        h = rms_norm(x, mix_norms[l, 0])
        y = token_mixing(h, w_in[l], mla_q_norm[l], mla_kv_norm[l], mla_wq_b[l], mla_wkv_b[l],
                         diff_lambda[l], w_branch[l], w_out[l], lambda_init,
                         ret_rope, mla_rope, diff_rope)
        x = x + rms_norm(y, mix_norms[l, 1])
        h = rms_norm(x, cross_norms[l, 0])
        m = rms_norm(mem, cross_norms[l, 2])
        x = x + rms_norm(cross_attention(h, m, cross_wq[l], cross_wkv[l], cross_wo[l]), cross_norms[l, 1])
        h = rms_norm(x, ffn2_norms[l, 0])
        x = x + 0.5 * rms_norm(swiglu(h, ffn2_w13[l], ffn2_w2[l]), ffn2_norms[l, 1])
    return x
```

```python
import math
import os
FSTAGE = int(os.environ.get('FSTAGE', '9'))
PSTAGE = int(os.environ.get('PSTAGE', '9'))
NTILE = int(os.environ.get('NTILE', '8'))
NOROPE = int(os.environ.get('NOROPE', '0'))
RSTAGE = int(os.environ.get('RSTAGE', '9'))
from contextlib import ExitStack
import numpy as np
import concourse.bass as bass
import concourse.mybir as mybir
from concourse.bass_utils import run_bass_kernel_spmd

ACT = mybir.ActivationFunctionType
ALU = mybir.AluOpType
AX = mybir.AxisListType
F32, BF16, I32 = mybir.dt.float32, mybir.dt.bfloat16, mybir.dt.int32

ENGS = ("pe", "act", "dve", "pool", "sp")
EIDX = {e: i for i, e in enumerate(ENGS)}
NE = len(ENGS)
EPS = 1e-6
PI = math.pi

S, D, NT = 2048, 1024, 16
DFF = 2816
NF = 22
INW = 6560


class Buf:
    __slots__ = ("name", "wE", "wD", "rE", "rD", "chan", "excl")

    def __init__(self, name, excl=False):
        self.name = name
        self.excl = excl
        self.wE = {}
        self.wD = []
        self.rE = {}
        self.rD = []
        self.chan = None


_BUFS = {}


def NB(name):
    b = _BUFS.get(name)
    if b is None:
        b = _BUFS[name] = Buf(name)
    return b


class Op:
    __slots__ = ("eng", "fn", "waits", "dwaits", "signal", "pos", "clock", "dma_chan", "ticket", "real")


class Prog:
    def __init__(self, nc):
        self.nc = nc
        self.ops = {e: [] for e in ENGS}
        self.known = {e: [-1] * NE for e in ENGS}
        self.dknown = {e: {} for e in ENGS}
        self.chans = {}
        self.nbuf = 0

    def _needE(self, eng, e2, p, op, same_ok):
        if e2 == eng and same_ok:
            return
        k = self.known[eng]
        i2 = EIDX[e2]
        if k[i2] >= p:
            return
        op.waits.append((e2, p))
        src = self.ops[e2][p]
        src.signal = True
        ck = src.clock
        for i in range(NE):
            if ck[i] > k[i]:
                k[i] = ck[i]
        k[i2] = p

    def _needD(self, eng, ch, v, op):
        dk = self.dknown[eng]
        if dk.get(ch, 0) >= v:
            return
        op.dwaits.append((ch, v))
        dk[ch] = v

    def add(self, eng, fn, reads=(), writes=(), dma=False, ndma=1, pe_acc=False, real=True, extra=()):
        op = Op()
        op.real = real
        op.eng = eng
        op.fn = fn
        op.waits = []
        op.dwaits = []
        op.signal = False
        op.pos = len(self.ops[eng])
        op.dma_chan = None
        op.ticket = None
        for e2, p in extra:
            self._needE(eng, e2, p, op, True)
        for b in reads:
            for e2, p in b.wE.items():
                self._needE(eng, e2, p, op, False)
            for ch, v in b.wD:
                self._needD(eng, ch, v, op)
            if b.excl:
                for e2, p in b.rE.items():
                    self._needE(eng, e2, p, op, True)
        for b in writes:
            for e2, p in b.wE.items():
                self._needE(eng, e2, p, op, pe_acc)
            for ch, v in b.wD:
                self._needD(eng, ch, v, op)
            for e2, p in b.rE.items():
                self._needE(eng, e2, p, op, eng == "pe")
            for ch, v in b.rD:
                self._needD(eng, ch, v, op)
        if dma:
            cb = writes[0] if writes else reads[0]
            if cb.chan is None:
                cb.chan = "c%d" % len(self.chans)
                self.chans[cb.chan] = 0
            self.chans[cb.chan] += 16 * ndma
            op.dma_chan = cb.chan
            ev = (cb.chan, self.chans[cb.chan])
        op.clock = list(self.known[eng])
        self.ops[eng].append(op)
        for b in writes:
            b.rE = {}
            b.rD = []
            if dma:
                b.wE = {}
                b.wD = [ev]
            else:
                if pe_acc and not b.wD and list(b.wE.keys()) == ["pe"]:
                    b.wE["pe"] = op.pos
                else:
                    b.wE = {eng: op.pos}
                    b.wD = []
        for b in reads:
            if b in writes:
                continue
            if dma:
                b.rD.append(ev)
            else:
                b.rE[eng] = op.pos

    def barrier(self):
        tg = []
        for e in ENGS:
            if e == "sp":
                continue
            for p in range(len(self.ops[e]) - 1, -1, -1):
                o = self.ops[e][p]
                if o.real and o.dma_chan is None:
                    tg.append((e, p))
                    break
        for e in ENGS:
            self.add(e, lambda en: en.nop(), real=False, extra=tg)

    def emit(self):
        nc = self.nc
        for e in ENGS:
            t = 0
            for op in self.ops[e]:
                if op.signal:
                    t += 1
                    op.ticket = t
        with ExitStack() as st:
            esem = {e: st.enter_context(nc.semaphore("s_" + e)) for e in ENGS if e != "sp"}
            csem = {c: st.enter_context(nc.semaphore(c)) for c in self.chans}
            allsems = list(esem.values()) + list(csem.values())
            with nc.Block() as pre:
                def clr(eng):
                    for sm in allsems:
                        eng.sem_clear(sm)
                pre.gpsimd(clr)
            block = st.enter_context(nc.Block())
            ops = self.ops

            def run(e, engobj):
                for op in ops[e]:
                    for (e2, p) in op.waits:
                        engobj.wait_ge(esem[e2], ops[e2][p].ticket)
                    for (ch, v) in op.dwaits:
                        engobj.wait_ge(csem[ch], v)
                    r = op.fn(engobj)
                    if op.dma_chan is not None:
                        if not isinstance(r, (list, tuple)):
                            r = [r]
                        for ins in r:
                            ins.then_inc(csem[op.dma_chan], 16)
                    elif op.signal:
                        r.then_inc(esem[e], 1)

            block.tensor(lambda eng: run("pe", eng))
            block.scalar(lambda eng: run("act", eng))
            block.vector(lambda eng: run("dve", eng))
            block.gpsimd(lambda eng: run("pool", eng))
            block.sync(lambda eng: run("sp", eng))


class Arena:
    def __init__(self, ap, nelem):
        self.ap = ap
        self.n = nelem
        self.off = 0

    def reset(self):
        self.off = 0

    def alloc(self, nelem, dt=BF16):
        n2 = nelem * (2 if dt == F32 else 1)
        n2 = (n2 + 31) // 32 * 32
        assert self.off + n2 <= self.n, ("arena overflow", self.off, n2, self.n)
        v = self.ap[:, self.off:self.off + n2]
        self.off += n2
        if dt == F32:
            v = v.bitcast(F32)
        return v[:, 0:nelem]


def bc_mid(ap2, n):
    return ap2.unsqueeze(2).to_broadcast([ap2.shape[0], ap2.shape[1], n])


def bc_in(ap2, a):
    return ap2.unsqueeze(1).to_broadcast([ap2.shape[0], a, ap2.shape[1]])


def host_consts():
    inv_ret = 1.0 / (np.float32(10000.0) ** (np.arange(0, 64, 2, dtype=np.float32) / np.float32(64)))
    inv_mla = 1.0 / (np.float32(500000.0) ** (np.arange(0, 32, 2, dtype=np.float32) / np.float32(32)))
    inv_dif = 1.0 / (np.float32(500000.0) ** (np.arange(0, 16, 2, dtype=np.float32) / np.float32(16)))
    inv = np.concatenate([inv_ret, inv_mla, inv_dif]).astype(np.float32)
    inv = np.tile(inv[None, :], (128, 1))
    gam = 1.0 - 2.0 ** (-5.0 - np.arange(4, dtype=np.float64))
    lg = np.log(gam)
    n = np.arange(128)
    dm = np.zeros((128, 4, 128), np.float64)
    for h in range(4):
        nn = n[None, :]
        mm = n[:, None]
        same = (nn // 64) == (mm // 64)
        later = (nn // 64) > (mm // 64)
        d_ = np.where(same, np.exp(lg[h] * np.abs(nn - mm)), np.where(later, np.exp(lg[h] * (nn - mm)), 0.0))
        dm[:, h, :] = d_ * 0.125
    qd = np.zeros((128, 2, 128), np.float64)
    for c in range(2):
        for j in range(2):
            qd[j * 64:(j + 1) * 64, c, :] = 0.125 * np.exp(lg[2 * c + j] * (n[None, :] + 1.0))
    kd = np.zeros((128, 4), np.float64)
    for h in range(4):
        kd[:, h] = np.exp(lg[h] * (127.0 - n))
    cd = np.zeros((128, 2), np.float64)
    for c in range(2):
        for j in range(2):
            cd[j * 64:(j + 1) * 64, c] = np.exp(lg[2 * c + j] * 128.0)
    rc = np.concatenate([dm.reshape(128, 512), qd.reshape(128, 256), kd, cd], axis=1).astype(np.float32)
    return inv, rc


RC_COLS = 512 + 256 + 4 + 2


def build_program(NL=4, stop=None, phases=None):
    _BUFS.clear()
    nc = bass.Bass("TRN2", target_bir_lowering=False)

    def din(name, shape, dt=F32):
        return nc.dram_tensor(name, list(shape), dt, kind="ExternalInput").ap()

    x_d = din("x", [S, D])
    mem_d = din("mem", [256, D])
    pos_d = din("pos", [128, NT], I32)
    inv_d = din("inv", [128, 56])
    rc_d = din("rc", [128, RC_COLS])
    W = {}
    for nm, shp in [("ffn1_norms", [4, 2, D]), ("ffn1_w13", [4, D, 2 * DFF]), ("ffn1_w2", [4, DFF, D]),
                    ("mix_norms", [4, 2, D]), ("w_in", [4, D, 3488]), ("wgate", [4, 8, 128, 3072]), ("wbr2", [4, 8, 128, 1536]), ("mla_q_norm", [4, 256]),
                    ("mla_kv_norm", [4, 128]), ("mla_wq_b", [4, 256, 768]), ("mla_wkv_b", [4, 128, 1024]),
                    ("diff_lambda", [4, 256]), ("w_out", [4, D, D]),
                    ("cross_norms", [4, 3, D]), ("cross_wq", [4, D, D]), ("cross_wkv", [4, D, 2 * D]),
                    ("cross_wo", [4, D, D]), ("ffn2_norms", [4, 2, D]), ("ffn2_w13", [4, D, 2 * DFF]),
                    ("ffn2_w2", [4, DFF, D])]:
        W[nm] = din(nm, shp)
    out_d = nc.dram_tensor("out", [S, D], F32, kind="ExternalOutput").ap()
    hTd = nc.dram_tensor("hT_scratch", [NT, 128, 8 * 128], BF16).ap()

    P = Prog(nc)
    st = ExitStack()
    with st:
        def sb(name, shape, dt):
            return st.enter_context(nc.sbuf_tensor(name, shape, dt))

        X = sb("X", [128, NT, D], F32)
        XB = [Buf("X%d" % t) for t in range(NT)]
        ident = sb("ident", [128, 128], BF16)
        Bident = Buf("ident")
        cosA = sb("cosA", [128, NT, 56], F32)
        sinA = sb("sinA", [128, NT, 56], F32)
        Btab = Buf("tab")
        rcs = sb("rcs", [128, RC_COLS], F32)
        Brc = Buf("rc")
        rstd_c = sb("rstd_c", [128, NT], F32)
        Brstd = [Buf("rstd%d" % t) for t in range(NT)]
        ARN = 65 * 1024
        arena_t = sb("arena", [128, ARN], BF16)
        A = Arena(arena_t, ARN)
        ps = [st.enter_context(nc.psum_tensor("ps%d" % i, [128, 512], F32)) for i in range(6)]
        ps += [st.enter_context(nc.psum_tensor("ps%d" % i, [128, 1024], BF16)) for i in (6, 7)]
        PB = [Buf("ps%d" % i, excl=True) for i in range(8)]

        def psb(i):
            return ps[i][:] if i >= 6 else ps[i][:].bitcast(BF16)

        def mm(out, lhsT, rhs, start, stop_, reads, wb):
            P.add("pe", lambda e: e.matmul(out, lhsT=lhsT, rhs=rhs, start=start, stop=stop_),
                  reads=reads, writes=[wb], pe_acc=True)

        def tr(out, in_, reads, wb):
            P.add("pe", lambda e: e.transpose(out=out, in_=in_, identity=ident[0:in_.shape[0], 0:in_.shape[0]]),
                  reads=list(reads) + [Bident], writes=[wb], pe_acc=True)

        def act(out, in_, func, reads, writes, scale=None, bias=None, accum=None):
            kw = {}
            if scale is not None:
                kw["scale"] = scale
            if bias is not None:
                kw["bias"] = bias
            if accum is not None:
                kw["accum_out"] = accum
            P.add("act", lambda e: e.activation(out=out, in_=in_, func=func, **kw), reads=reads, writes=writes)

        def tt(eng, out, in0, in1, op, reads, writes):
            P.add(eng, lambda e: e.tensor_tensor(out=out, in0=in0, in1=in1, op=op), reads=reads, writes=writes)

        def ts(eng, out, in0, s1, s2, op0, op1, reads, writes):
            if op1 is None:
                s2, op1 = 0.0, ALU.add
            P.add(eng, lambda e: e.tensor_scalar(out=out, in0=in0, scalar1=s1, scalar2=s2, op0=op0, op1=op1),
                  reads=reads, writes=writes)

        def stt(eng, out, in0, scalar, in1, op0, op1, reads, writes):
            P.add(eng, lambda e: e.scalar_tensor_tensor(out=out, in0=in0, scalar=scalar, in1=in1, op0=op0, op1=op1),
                  reads=reads, writes=writes)

        def cp(eng, out, in_, reads, writes):
            if eng == "act":
                P.add("act", lambda e: e.copy(out=out, in_=in_), reads=reads, writes=writes)
            else:
                P.add(eng, lambda e: e.tensor_copy(out=out, in_=in_), reads=reads, writes=writes)

        def recip(out, in_, reads, writes):
            P.add("dve", lambda e: e.reciprocal(out=out, in_=in_), reads=reads, writes=writes)

        def memset(eng, ap, val, writes):
            P.add(eng, lambda e: e.memset(ap, val), writes=writes)

        def dma(eng, out, in_, reads, writes):
            P.add(eng, lambda e: e.dma_start(out=out, in_=in_), reads=reads, writes=writes, dma=True)

        def wload(out, in_, wb):
            dma("pool", out, in_, [], [wb])

        def kview(w2d):
            return w2d.rearrange("(k p) n -> p k n", p=128)

        stat = sb("stat", [128, 64], F32)
        Bstat = [Buf("stat%d" % i) for i in range(16)]
        stat_i = [0]

        def stat_slot():
            i = stat_i[0] % 16
            stat_i[0] += 1
            return stat[:, 4 * i:4 * i + 4], Bstat[i]

        epsT = sb("epsT", [128, 1], F32)
        Beps = Buf("eps")
        memset("pool", epsT[:], EPS, [Beps])
        junk = sb("junk", [128, D], BF16)
        Bjunk = Buf("junk")

        def rstd_of(src, n, reads, out_ap, out_buf, ncols=1):
            sl, sbuf_ = stat_slot()
            act(junk[:, 0:n], src, ACT.Square, reads, [sbuf_], accum=sl[:, 0:1])
            act(sl[:, 1:2], sl[:, 0:1], ACT.Ln, [sbuf_, Beps], [sbuf_], scale=1.0 / n, bias=epsT[:, 0:1])
            act(out_ap, sl[:, 1:2], ACT.Exp, [sbuf_], [out_buf], scale=-0.5)

        memset("pool", ident[:], 0.0, [Bident])
        P.add("pool", lambda e: e.affine_select(out=ident[:], in_=ident[:], pattern=[[-1, 128]],
                                               compare_op=ALU.not_equal, fill=1.0, base=0, channel_multiplier=1),
              reads=[Bident], writes=[Bident])
        for t in range(NT):
            dma("sp", X[:, t, :], x_d[t * 128:(t + 1) * 128, :], [], [XB[t]])
        dma("sp", rcs[:], rc_d, [], [Brc])
        A.reset()
        posi = sb("posi", [128, NT], I32)
        posf = sb("posf", [128, NT], F32)
        invs = sb("invs", [128, 56], F32)
        ang = A.alloc(NT * 56, F32).rearrange("p (t i) -> p t i", i=56)
        tmpa = A.alloc(NT * 56, F32).rearrange("p (t i) -> p t i", i=56)
        Bpos, Binv, Bang, Btmpa = Buf("pos"), Buf("inv"), Buf("ang"), Buf("tmpa")
        dma("sp", posi[:], pos_d, [], [Bpos])
        dma("sp", invs[:], inv_d, [], [Binv])
        cp("dve", posf[:], posi[:], [Bpos], [Bpos])
        tt("dve", ang, bc_mid(posf[:], 56), bc_in(invs[:], NT), ALU.mult, [Bpos, Binv], [Bang])
        tmpi = A.alloc(NT * 56, F32).bitcast(I32).rearrange("p (t i) -> p t i", i=56)
        tmpf = A.alloc(NT * 56, F32).rearrange("p (t i) -> p t i", i=56)
        tmpg = A.alloc(NT * 56, F32).rearrange("p (t i) -> p t i", i=56)
        Btmpi, Btmpf, Btmpg = Buf("tmpi"), Buf("tmpf"), Buf("tmpg")
        for tab, ph in (() if NOROPE else ((sinA, 0.0), (cosA, 0.25))):
            ts("dve", tmpa, ang, 1.0 / (2.0 * PI), ph, ALU.mult, ALU.add, [Bang], [Btmpa])
            cp("dve", tmpi, tmpa, [Btmpa], [Btmpi])
            cp("dve", tmpf, tmpi, [Btmpi], [Btmpf])
            tt("dve", tmpa, tmpa, tmpf, ALU.subtract, [Btmpa, Btmpf], [Btmpa])
            ts("dve", tmpg, tmpa, 0.5, 1.0, ALU.is_gt, ALU.mult, [Btmpa], [Btmpg])
            tt("dve", tmpa, tmpa, tmpg, ALU.subtract, [Btmpa, Btmpg], [Btmpa])
            ts("dve", tmpg, tmpa, -0.5, 1.0, ALU.is_lt, ALU.mult, [Btmpa], [Btmpg])
            tt("dve", tmpa, tmpa, tmpg, ALU.add, [Btmpa, Btmpg], [Btmpa])
            ts("dve", tmpa, tmpa, 0.4999999, -0.4999999, ALU.min, ALU.max, [Btmpa], [Btmpa])
            act(tab[:], tmpa, ACT.Sin, [Btmpa], [Btab], scale=2.0 * PI)
        cosr, sinr = cosA[:, :, 0:32], sinA[:, :, 0:32]
        cosm, sinm = cosA[:, :, 32:48], sinA[:, :, 32:48]
        cosd, sind = cosA[:, :, 48:56], sinA[:, :, 48:56]
        dmaskT = rcs[:, 0:512].rearrange("p (h n) -> p h n", n=128)
        qdecT = rcs[:, 512:768].rearrange("p (c n) -> p c n", n=128)
        kdec = rcs[:, 768:772]
        cdcol = rcs[:, 772:774]
        P.barrier()

        def load_gbc(dst, vec_ap, buf):
            dma("sp", dst, vec_ap.partition_broadcast(128), [], [buf])

        def prenorm_tile(t, gbc, Bg, hb, Bhb, hT_out, BhT, pbank, cache=True, eng_evac="dve"):
            if not cache:
                rstd_of(X[:, t, :], D, [XB[t]], rstd_c[:, t:t + 1], Brstd[t])
            if PSTAGE < 2:
                return
            stt("dve", hb, X[:, t, :], rstd_c[:, t:t + 1], gbc, ALU.mult, ALU.mult, [XB[t], Brstd[t], Bg], [Bhb])
            if PSTAGE < 3:
                cp("dve", X[:, t, :], hb, [Bhb], [XB[t]])
                return
            pv = psb(pbank).rearrange("p (k n) -> p k n", n=128)
            for k in range(8):
                tr(pv[:, k, :], hb[:, k * 128:(k + 1) * 128], [Bhb], PB[pbank])
            if PSTAGE != 4:
                cp(eng_evac, hT_out, pv, [PB[pbank]], [BhT])

        def interleave(gens, W):
            gens = list(gens)
            active = []
            while gens or active:
                while gens and len(active) < W:
                    active.append(gens.pop(0))
                for g_ in list(active):
                    try:
                        next(g_)
                    except StopIteration:
                        active.remove(g_)

        def prenorm_gen(t, gbc, Bg, hb, Bhb, hT_out, BhT, pbank, cache=True, eng_evac="dve"):
            if not cache:
                rstd_of(X[:, t, :], D, [XB[t]], rstd_c[:, t:t + 1], Brstd[t])
                yield
            stt("dve", hb, X[:, t, :], rstd_c[:, t:t + 1], gbc, ALU.mult, ALU.mult, [XB[t], Brstd[t], Bg], [Bhb])
            yield
            pv = psb(pbank).rearrange("p (k n) -> p k n", n=128)
            for k in range(8):
                tr(pv[:, k, :], hb[:, k * 128:(k + 1) * 128], [Bhb], PB[pbank])
            cp(eng_evac, hT_out, pv, [PB[pbank]], [BhT])
            yield

        def postnorm_add(t, ybanks, gbc, Bg, half):
            sl, sbuf_ = stat_slot()
            for i, b in enumerate(ybanks):
                act(junk[:, i * 512:(i + 1) * 512], ps[b][:], ACT.Square, [PB[b]], [sbuf_], accum=sl[:, i:i + 1])
            tt("dve", sl[:, 2:3], sl[:, 0:1], sl[:, 1:2], ALU.add, [sbuf_], [sbuf_])
            act(sl[:, 3:4], sl[:, 2:3], ACT.Ln, [sbuf_, Beps], [sbuf_], scale=1.0 / D, bias=epsT[:, 0:1])
            act(sl[:, 0:1], sl[:, 3:4], ACT.Exp, [sbuf_], [sbuf_], scale=-0.5)
            for i, b in enumerate(ybanks):
                stt("dve", ps[b][:], ps[b][:], sl[:, 0:1], gbc[:, i * 512:(i + 1) * 512], ALU.mult, ALU.mult,
                    [sbuf_, Bg], [PB[b]])
                stt("dve", X[:, t, i * 512:(i + 1) * 512], ps[b][:], 0.5 if half else 1.0,
                    X[:, t, i * 512:(i + 1) * 512], ALU.mult, ALU.add, [PB[b]], [XB[t]])

        def ffn(l, w13, w2, norms):
            for hh in range(2):
                A.reset()
                hb = [A.alloc(D) for _ in range(4)]
                Bhb = [NB("hb%d" % i) for i in range(4)]
                sg = [A.alloc(512) for _ in range(2)]
                Bsg = [NB("sg%d" % i) for i in range(2)]
                g1 = A.alloc(D, F32)
                g2 = A.alloc(D, F32)
                Bg1, Bg2 = NB("g1"), NB("g2")
                load_gbc(g1, norms[l, 0, :], Bg1)
                load_gbc(g2, norms[l, 1, :], Bg2)
                a0 = A.off
                hT = A.alloc(8 * 1024).rearrange("p (k n) -> p k n", n=1024)
                BhT = [NB("hT%d" % i) for i in range(8)]
                NWB = 3
                wgu = [A.alloc(2 * 8 * 256).rearrange("p (s k n) -> p s k n", s=2, k=8) for _ in range(NWB)]
                Bwgu = [NB("wgu%d" % i) for i in range(NWB)]
                w2hi = arena_t[:, a0:a0 + 11 * 1024].rearrange("p (f n) -> p f n", n=1024)
                w2lo = A.alloc(11 * 1024).rearrange("p (f n) -> p f n", n=1024)
                w2p = (w2lo, w2hi)
                actT = A.alloc(NF * 1024).rearrange("p (f n) -> p f n", n=1024)
                BaT = [[NB("aT%d_%d" % (f, g)) for g in range(2)] for f in range(NF)]
                Bw2 = [NB("w2_%d" % i) for i in range(2)]
                interleave([prenorm_gen(hh * 8 + i, g1, Bg1, hb[i % 4], Bhb[i % 4], hT[:, :, i * 128:(i + 1) * 128],
                                        BhT[i], pbank=(6, 7, 0, 1)[i % 4], cache=False, eng_evac="act" if i % 2 else "dve")
                            for i in range(NTILE)], 4)
                if FSTAGE < 2:
                    P.barrier()
                    continue
                w2v = w2[l].rearrange("(f p) n -> p f n", p=128)
                wload(w2lo, w2v[:, 0:11, :], Bw2[0])
                w13v = kview(w13[l])
                cnt = 0
                for fg in range(11):
                    wb = fg % NWB
                    P.add("pool", lambda e, wb=wb, fg=fg: [
                        e.dma_start(out=wgu[wb][:, 0], in_=w13v[:, :, fg * 256:(fg + 1) * 256]),
                        e.dma_start(out=wgu[wb][:, 1], in_=w13v[:, :, DFF + fg * 256:DFF + (fg + 1) * 256])],
                        writes=[Bwgu[wb]], dma=True, ndma=2)
                    for fi in range(2):
                        f = fg * 2 + fi
                        for g in range(2):
                            bg = (cnt % 3) * 2
                            bu = bg + 1
                            cnt += 1
                            for k in range(8):
                                mm(ps[bg][:], wgu[wb][:, 0, k, fi * 128:(fi + 1) * 128], hT[:, k, g * 512:(g + 1) * 512],
                                   k == 0, k == 7, [Bwgu[wb]] + BhT[g * 4:(g + 1) * 4], PB[bg])
                            for k in range(8):
                                mm(ps[bu][:], wgu[wb][:, 1, k, fi * 128:(fi + 1) * 128], hT[:, k, g * 512:(g + 1) * 512],
                                   k == 0, k == 7, [Bwgu[wb]] + BhT[g * 4:(g + 1) * 4], PB[bu])
                            si = cnt % 2
                            act(sg[si], ps[bg][:], ACT.Silu, [PB[bg]], [Bsg[si]])
                            tt("dve", actT[:, f, g * 512:(g + 1) * 512], sg[si], ps[bu][:], ALU.mult,
                               [Bsg[si], PB[bu]], [BaT[f][g]])
                if FSTAGE < 3:
                    P.barrier()
                    continue
                P.add("pool", lambda e: e.dma_start(out=w2hi, in_=w2v[:, 11:22, :]),
                      writes=[Bw2[1]] + BhT + [Bwgu[0]], dma=True)
                def ybank(i):
                    return ((i % 3) * 2, (i % 3) * 2 + 1)

                def down_mm(i, f0, f1):
                    yb = ybank(i)
                    for oh in range(2):
                        for f in range(f0, f1):
                            mm(ps[yb[oh]][:], actT[:, f, i * 128:(i + 1) * 128], w2p[f // 11][:, f % 11, oh * 512:(oh + 1) * 512],
                               f == 0, f == NF - 1, [BaT[f][i // 4], Bw2[f // 11]], PB[yb[oh]])

                for i in range(3):
                    down_mm(i, 0, 11)
                for i in range(8):
                    t = hh * 8 + i
                    yb = ybank(i)
                    if i < 3:
                        down_mm(i, 11, NF)
                    else:
                        down_mm(i, 0, NF)
                    postnorm_add(t, yb, g2, Bg2, True)
                P.barrier()


        def v3(ap2, n):
            return ap2.rearrange("p (a n) -> p a n", n=n)

        def cross(l):
            A.reset()
            cn = W["cross_norms"]
            hb = [A.alloc(D) for _ in range(3)]
            Bhb = [NB("hb0"), NB("hb1"), NB("hb2")]
            g1, g2, g3 = A.alloc(D, F32), A.alloc(D, F32), A.alloc(D, F32)
            Bg1, Bg2, Bg3 = NB("g1"), NB("g2"), NB("g3")
            load_gbc(g1, cn[l, 0, :], Bg1)
            load_gbc(g2, cn[l, 1, :], Bg2)
            load_gbc(g3, cn[l, 2, :], Bg3)
            wq = v3(A.alloc(8 * 1024), 1024)
            wo = v3(A.alloc(8 * 1024), 1024)
            Bwq, Bwo = NB("cwq"), NB("cwo")
            wload(wq, kview(W["cross_wq"][l]), Bwq)
            wkv = [v3(A.alloc(8 * 512), 512) for _ in range(2)]
            Bwkv = [NB("wkv0"), NB("wkv1")]
            memf = A.alloc(D, F32)
            Bmemf = NB("memf")
            memT = v3(A.alloc(8 * 256), 256)
            BmemT = NB("memT")
            kTc = v3(A.alloc(8 * 256), 256)
            BkTc = NB("kTc")
            Vc = v3(A.alloc(2 * 1024), 1024)
            BVc = NB("Vc")
            hTt = [v3(A.alloc(8 * 128), 128) for _ in range(3)]
            BhTt = [NB("hTt0"), NB("hTt1"), NB("hTt2")]
            qTt = [v3(A.alloc(8 * 128), 128) for _ in range(3)]
            BqTt = [NB("qTt0"), NB("qTt1"), NB("qTt2")]
            Pb = [v3(A.alloc(4 * 256), 256) for _ in range(3)]
            BPb = [NB("Pb0"), NB("Pb1"), NB("Pb2")]
            PT = [v3(A.alloc(8 * 128), 128) for _ in range(3)]
            BPT = [NB("PT0"), NB("PT1"), NB("PT2")]
            ob = [A.alloc(D) for _ in range(3)]
            Bob = [NB("ob0"), NB("ob1"), NB("ob2")]
            oT = [v3(A.alloc(8 * 128), 128) for _ in range(3)]
            BoT = [NB("oT0"), NB("oT1"), NB("oT2")]
            pv6 = v3(psb(6), 128)
            pv7 = v3(psb(7), 128)
            for mt in range(2):
                dma("sp", memf, mem_d[mt * 128:(mt + 1) * 128, :], [], [Bmemf])
                so, sbo = stat_slot()
                rstd_of(memf, D, [Bmemf], so[:, 0:1], sbo)
                stt("dve", hb[mt], memf, so[:, 0:1], g3, ALU.mult, ALU.mult, [Bmemf, sbo, Bg3], [Bhb[mt]])
                for k in range(8):
                    tr(pv6[:, k, :], hb[mt][:, k * 128:(k + 1) * 128], [Bhb[mt]], PB[6])
                cp("dve", memT[:, :, mt * 128:(mt + 1) * 128], pv6, [PB[6]], [BmemT])
            wkvv = kview(W["cross_wkv"][l])
            for cg in range(4):
                wb = cg % 2
                wload(wkv[wb], wkvv[:, :, cg * 512:(cg + 1) * 512], Bwkv[wb])
                if cg < 2:
                    for q in range(4):
                        hj = cg * 4 + q
                        bank = q % 2
                        for k in range(8):
                            mm(ps[bank][:, 0:256], wkv[wb][:, k, q * 128:(q + 1) * 128], memT[:, k, :],
                               k == 0, k == 7, [Bwkv[wb], BmemT], PB[bank])
                        cp("act", kTc[:, hj, :], ps[bank][:, 0:256], [PB[bank]], [BkTc])
                else:
                    oh = cg - 2
                    for mt in range(2):
                        bank = 2 + mt
                        for k in range(8):
                            mm(ps[bank][:], memT[:, k, mt * 128:(mt + 1) * 128], wkv[wb][:, k, :],
                               k == 0, k == 7, [Bwkv[wb], BmemT], PB[bank])
                        cp("act", Vc[:, mt, oh * 512:(oh + 1) * 512], ps[bank][:], [PB[bank]], [BVc])
            wload(wo, kview(W["cross_wo"][l]), Bwo)
            def c_tile(t):
                w = t % 2
                b0, b1 = 2 * w, 2 * w + 1
                yield from prenorm_gen(t, g1, Bg1, hb[w], Bhb[w], hTt[w], BhTt[w], pbank=6, cache=False)
                for hj in range(8):
                    bank = b0 + hj // 4
                    o = ps[bank][:, (hj % 4) * 128:(hj % 4 + 1) * 128]
                    for k in range(8):
                        mm(o, wq[:, k, hj * 128:(hj + 1) * 128], hTt[w][:, k, :], k == 0, k == 7,
                           [Bwq, BhTt[w]], PB[bank])
                    if hj % 4 == 3:
                        yield
                for i, bank in enumerate((b0, b1)):
                    cp("act" if i else "dve", qTt[w][:, i * 4:(i + 1) * 4, :], v3(ps[bank][:], 128),
                       [PB[bank]], [BqTt[w]])
                yield
                for h in range(4):
                    bank = b0 + h // 2
                    o = ps[bank][:, (h % 2) * 256:(h % 2 + 1) * 256]
                    for j in range(2):
                        mm(o, qTt[w][:, 2 * h + j, :], kTc[:, 2 * h + j, :], j == 0, j == 1, [BqTt[w], BkTc], PB[bank])
                yield
                sl, sb_ = stat_slot()
                sl2, sb2 = stat_slot()
                for i, bank in enumerate((b0, b1)):
                    P.add("dve", lambda e, bank=bank, sl=sl, i=i: e.reduce_max(
                        out=sl[:, 2 * i:2 * i + 2], in_=v3(ps[bank][:], 256), axis=AX.X),
                        reads=[PB[bank]], writes=[sb_])
                ts("dve", sl, sl, -1.0 / 16, None, ALU.mult, None, [sb_], [sb_])
                yield
                for h in range(4):
                    bank = b0 + h // 2
                    act(Pb[w][:, h, :], ps[bank][:, (h % 2) * 256:(h % 2 + 1) * 256], ACT.Exp, [PB[bank], sb_],
                        [BPb[w], sb2], scale=1.0 / 16, bias=sl[:, h:h + 1], accum=sl2[:, h:h + 1])
                yield
                recip(sl2, sl2, [sb2], [sb2])
                for h in range(4):
                    for m in range(2):
                        tr(pv7[:, 2 * h + m, :], Pb[w][:, h, m * 128:(m + 1) * 128], [BPb[w]], PB[7])
                cp("dve", PT[w], pv7, [PB[7]], [BPT[w]])
                yield
                for h in range(4):
                    bank = b0 + h // 2
                    o = ps[bank][:, (h % 2) * 256:(h % 2 + 1) * 256]
                    for m in range(2):
                        mm(o, PT[w][:, 2 * h + m, :], Vc[:, m, h * 256:(h + 1) * 256], m == 0, m == 1, [BPT[w], BVc], PB[bank])
                yield
                for i, bank in enumerate((b0, b1)):
                    tt("dve", v3(ob[w][:, i * 512:(i + 1) * 512], 256), v3(ps[bank][:], 256),
                       bc_mid(sl2[:, 2 * i:2 * i + 2], 256), ALU.mult, [PB[bank], sb2], [Bob[w]])
                yield
                for k in range(8):
                    tr(pv6[:, k, :], ob[w][:, k * 128:(k + 1) * 128], [Bob[w]], PB[6])
                cp("act", oT[w], pv6, [PB[6]], [BoT[w]])
                yield
                for oh, bank in enumerate((b0, b1)):
                    for k in range(8):
                        mm(ps[bank][:], oT[w][:, k, :], wo[:, k, oh * 512:(oh + 1) * 512], k == 0, k == 7, [BoT[w], Bwo], PB[bank])
                    yield
                postnorm_add(t, (b0, b1), g2, Bg2, False)
                yield

            interleave([c_tile(t) for t in range(NT)], 2)
            P.barrier()

        dbgflag = [False]
        BhTd = [Buf("hTd%d" % t) for t in range(NT)]

        def hT_store(t, src3, Bsrc):
            dma("sp", hTd[t], src3.rearrange("p k n -> p (k n)"), [Bsrc], [BhTd[t]])

        def hT_load(t, dst3, Bdst):
            dma("sp", dst3, hTd[t].rearrange("p (k n) -> p k n", n=128), [BhTd[t]], [Bdst])

        def mixer(l, dbg=None):
            A.reset()
            mn = W["mix_norms"]
            g1 = A.alloc(D, F32)
            Bg1 = NB("g1")
            load_gbc(g1, mn[l, 0, :], Bg1)
            yT = v3(A.alloc(12 * 2048), 2048)
            ByT = [[NB("yT%d_%d" % (b, t)) for t in range(NT)] for b in range(3)]
            hb = [A.alloc(D) for _ in range(2)]
            Bhb = [NB("hb0"), NB("hb1")]
            hTt = [v3(A.alloc(8 * 128), 128) for _ in range(2)]
            BhTt = [NB("hTt0"), NB("hTt1")]
            mark = A.off
            win = kview(W["w_in"][l])
            pv6 = v3(psb(6), 128)
            pv7 = v3(psb(7), 128)

            def hT_tile(t, first):
                i2 = t % 2
                prenorm_tile(t, g1, Bg1, hb[i2], Bhb[i2], hTt[i2], BhTt[i2], pbank=6, cache=not first)
                return hTt[i2], BhTt[i2]

            Wr = v3(A.alloc(8 * 1536), 1536)
            BWr = [NB("Wr0"), NB("Wr1"), NB("Wr2")]
            for i in range(3):
                wload(Wr[:, :, i * 512:(i + 1) * 512], win[:, :, i * 512:(i + 1) * 512], BWr[i])
            St = v3(A.alloc(2 * 128, F32), 128)
            BSt = NB("St")
            Sb = [v3(A.alloc(2 * 128), 128) for _ in range(3)]
            BSb = [NB("Sb%d" % i) for i in range(3)]
            memset("pool", St, 0.0, [BSt])

            def hv(ap3, j):
                return ap3.rearrange("p (c j) n -> p c j n", j=2)[:, :, j, :]

            RS = []
            for w in range(2):
                d_ = {}
                d_["qkf"] = A.alloc(512, F32)
                for nm in ("ta", "tb", "tc", "td"):
                    d_[nm] = A.alloc(256, F32)
                d_["qkr"] = A.alloc(512)
                d_["kdt"] = A.alloc(256)
                d_["vt"] = A.alloc(512)
                d_["kT"] = v3(A.alloc(2 * 128), 128)
                d_["qpad"] = v3(A.alloc(4 * 128), 128)
                d_["qdpad"] = v3(A.alloc(4 * 128), 128)
                d_["AT"] = v3(A.alloc(4 * 128), 128)
                d_["on"] = A.alloc(512, F32)
                d_["sgr"] = A.alloc(512, F32)
                d_["yb"] = A.alloc(512)
                d_["B"] = {k: NB("r%s%d" % (k, w)) for k in ("qkf", "ta", "tb", "tc", "td", "qkr1", "qkr2", "kdt", "vt",
                                                              "kT", "qpad", "qdpad", "AT", "on", "sgr", "yb")}
                memset("pool", d_["qpad"], 0.0, [d_["B"]["qpad"]])
                memset("pool", d_["qdpad"], 0.0, [d_["B"]["qdpad"]])
                RS.append(d_)

            def r_tile(t):
                w = t % 2
                R_ = RS[w]
                B_ = R_["B"]
                bA, bB, bC = 3 * w, 3 * w + 1, 3 * w + 2
                pvT = pv7 if w == 0 else pv6
                PBT = PB[7] if w == 0 else PB[6]
                yield from prenorm_gen(t, g1, Bg1, hb[w], Bhb[w], hTt[w], BhTt[w], pbank=6, cache=False)
                hT_, BhT_ = hTt[w], BhTt[w]
                hT_store(t, hT_, BhT_)
                for (bank, c0) in ((bA, 0), (bB, 512), (bC, 1024)):
                    for k in range(8):
                        mm(ps[bank][:], hT_[:, k, :], Wr[:, k, c0:c0 + 512], k == 0, k == 7,
                           [BhT_, BWr[c0 // 512]], PB[bank])
                    yield
                qkf, qkr, kdt, vt = R_["qkf"], R_["qkr"], R_["kdt"], R_["vt"]
                cp("act", qkf, ps[bA][:], [PB[bA]], [B_["qkf"]])
                cp("act", vt, ps[bB][:], [PB[bB]], [B_["vt"]])
                act(R_["sgr"], ps[bC][:], ACT.Silu, [PB[bC]], [B_["sgr"]])
                yield
                q3 = v3(qkf, 64)
                x1, x2 = q3[:, :, 0:32], q3[:, :, 32:64]
                cs, sn = bc_in(cosr[:, t, :], 8), bc_in(sinr[:, t, :], 8)
                o3 = v3(qkr, 64)
                ta, tb, tc_, td = R_["ta"], R_["tb"], R_["tc"], R_["td"]
                tt("dve", v3(ta, 32), x1, cs, ALU.mult, [B_["qkf"], Btab], [B_["ta"]])
                tt("dve", v3(tb, 32), x2, sn, ALU.mult, [B_["qkf"], Btab], [B_["tb"]])
                tt("pool", v3(tc_, 32), x2, cs, ALU.mult, [B_["qkf"], Btab], [B_["tc"]])
                tt("pool", v3(td, 32), x1, sn, ALU.mult, [B_["qkf"], Btab], [B_["td"]])
                yield
                tt("dve", o3[:, :, 0:32], v3(ta, 32), v3(tb, 32), ALU.subtract, [B_["ta"], B_["tb"]], [B_["qkr1"]])
                tt("pool", o3[:, :, 32:64], v3(tc_, 32), v3(td, 32), ALU.add, [B_["tc"], B_["td"]], [B_["qkr2"]])
                yield
                tt("dve", v3(kdt, 64), v3(qkr[:, 256:512], 64), bc_mid(kdec, 64), ALU.mult,
                   [B_["qkr1"], B_["qkr2"], Brc], [B_["kdt"]])
                for c in range(4):
                    tr(pvT[:, c, :], qkr[:, c * 128:(c + 1) * 128], [B_["qkr1"], B_["qkr2"]], PBT)
                cp("act", R_["kT"], pvT[:, 2:4, :], [PBT], [B_["kT"]])
                for h in range(4):
                    j, c = h % 2, h // 2
                    pr = slice(j * 64, (j + 1) * 64)
                    cp("dve" if h % 2 else "act", R_["qpad"][pr, h, :], pvT[pr, c, :], [PBT], [B_["qpad"]])
                    tt("dve", R_["qdpad"][pr, h, :], pvT[pr, c, :], qdecT[pr, c, :], ALU.mult, [PBT, Brc], [B_["qdpad"]])
                yield
                KV = ps[bC][:].rearrange("p (c j n) -> p c j n", c=2, j=2)
                for c in range(2):
                    for j in range(2):
                        mm(KV[:, c, j, :], kdt[:, c * 128:(c + 1) * 128], vt[:, (2 * c + j) * 128:(2 * c + j + 1) * 128],
                           True, True, [B_["kdt"], B_["vt"]], PB[bC])
                for c in range(2):
                    for j in range(2):
                        pr = slice(j * 64, (j + 1) * 64)
                        stt("dve", St[pr, c, :], St[pr, c, :], cdcol[pr, c:c + 1], KV[pr, c, j, :], ALU.mult, ALU.add,
                            [BSt, PB[bC], Brc], [BSt])
                cp("act", Sb[(t + 1) % 3], St, [BSt], [BSb[(t + 1) % 3]])
                yield
                S3 = v3(ps[bA][:], 128)
                for h in range(4):
                    mm(S3[:, h, :], R_["kT"][:, h // 2, :], R_["qpad"][:, h, :], True, True, [B_["kT"], B_["qpad"]], PB[bA])
                yield
                tt("dve", R_["AT"], S3, dmaskT, ALU.mult, [PB[bA], Brc], [B_["AT"]])
                yield
                O3 = v3(ps[bB][:], 128)
                for h in range(4):
                    mm(O3[:, h, :], R_["AT"][:, h, :], vt[:, h * 128:(h + 1) * 128], True, t == 0, [B_["AT"], B_["vt"]], PB[bB])
                    if t > 0:
                        mm(O3[:, h, :], R_["qdpad"][:, h, :], Sb[t % 3][:, h // 2, :], False, True,
                           [B_["qdpad"], BSb[t % 3]], PB[bB])
                yield
                sl, sb_ = stat_slot()
                for h in range(4):
                    act(junk[:, 0:128], O3[:, h, :], ACT.Square, [PB[bB]], [sb_], accum=sl[:, h:h + 1])
                so, sbo = stat_slot()
                act(sl, sl, ACT.Ln, [sb_, Beps], [sb_], scale=1.0 / 128, bias=epsT[:, 0:1])
                act(so, sl, ACT.Exp, [sb_], [sbo], scale=-0.5)
                yield
                tt("dve", v3(R_["on"], 128), O3, bc_mid(so, 128), ALU.mult, [PB[bB], sbo], [B_["on"]])
                yield
                tt("pool", R_["yb"], R_["on"], R_["sgr"], ALU.mult, [B_["on"], B_["sgr"]], [B_["yb"]])
                yield
                for c in range(4):
                    tr(pvT[:, 4 + c, :], R_["yb"][:, c * 128:(c + 1) * 128], [B_["yb"]], PBT)
                cp("act", yT[:, 0:4, t * 128:(t + 1) * 128], pvT[:, 4:8, :], [PBT], [ByT[0][t]])
                if dbg == "ret":
                    dbgflag[0] = True
                    dma("pool", out_d[t * 128:(t + 1) * 128, 0:512], R_["yb"], [B_["yb"]], [NB("dbgout")])
                yield

            interleave([r_tile(t) for t in range(NT)], 2)
            P.barrier()
            A.off = mark
            if dbg == "ret":
                return

            lam_i = 0.8 - 0.6 * math.exp(-0.3 * l)
            dlb = A.alloc(256, F32)
            Bdlb = NB("dlb")
            load_gbc(dlb, W["diff_lambda"][l, :], Bdlb)
            lsl, Blsl = A.alloc(8, F32), NB("lsl")
            dtmp = A.alloc(128, F32)
            Bdtmp = NB("dtmp")
            tt("dve", v3(dtmp, 64), v3(dlb, 64)[:, 0:4:2, :], v3(dlb, 64)[:, 1:4:2, :], ALU.mult, [Bdlb], [Bdtmp])
            P.add("dve", lambda e: e.reduce_sum(out=lsl[:, 0:2], in_=v3(dtmp, 64), axis=AX.X), reads=[Bdtmp], writes=[Blsl])
            act(lsl[:, 2:4], lsl[:, 0:2], ACT.Exp, [Blsl], [Blsl])
            tt("dve", lsl[:, 4:5], lsl[:, 3:4], lsl[:, 2:3], ALU.subtract, [Blsl], [Blsl])
            ts("dve", lsl[:, 5:6], lsl[:, 4:5], -lam_i, None, ALU.add, None, [Blsl], [Blsl])
            neglam = lsl[:, 5:6]
            mark2 = A.off
            for hp in ((1,) if os.environ.get('HP1') == '1' else ((1, 0) if os.environ.get('HP1') == '2' else ((0, 0) if os.environ.get('HP1') == '3' else range(2)))):
                A.off = mark2
                Wd = v3(A.alloc(8 * 768), 768)
                BWd = NB("Wd")
                P.add("pool", lambda e, hp=hp: [
                    e.dma_start(out=Wd[:, :, i * 256:(i + 1) * 256],
                                in_=win[:, :, 1952 + i * 512 + hp * 256:1952 + i * 512 + hp * 256 + 256]) for i in range(3)],
                    writes=[BWd], dma=True, ndma=3)
                qpad = A.alloc(2 * 2 * 2048).rearrange("p (h j n) -> p h j n", h=2, j=2)
                kTd = v3(A.alloc(2 * 2048), 2048)
                Vaug = A.alloc(NT * 2 * 130).rearrange("p (t h n) -> p t h n", t=NT, h=2)
                Bq = [NB("dq%d" % t) for t in range(NT)]
                Bk = [NB("dk%d" % t) for t in range(NT)]
                Bv = [NB("dv%d" % t) for t in range(NT)]
                Bones = NB("dones")
                qr, kr = [A.alloc(256) for _ in range(2)], [A.alloc(256) for _ in range(2)]
                Bqr, Bkr = [NB("dqr0"), NB("dqr1")], [NB("dkr0"), NB("dkr1")]
                ra, rb = [A.alloc(32, F32) for _ in range(2)], [A.alloc(32, F32) for _ in range(2)]
                Bra, Brb = [NB("dra0"), NB("dra1")], [NB("drb0"), NB("drb1")]
                rc, rd = [A.alloc(32, F32) for _ in range(2)], [A.alloc(32, F32) for _ in range(2)]
                Brc_, Brd = [NB("drc0"), NB("drc1")], [NB("drd0"), NB("drd1")]
                Pt = [A.alloc(512) for _ in range(4)]
                BPt = [NB("dP%d" % i) for i in range(4)]
                o1, o2 = [A.alloc(128, F32) for _ in range(4)], [A.alloc(128, F32) for _ in range(4)]
                Bo1, Bo2 = [NB("do1%d" % i) for i in range(4)], [NB("do2%d" % i) for i in range(4)]
                yh = [A.alloc(128) for _ in range(4)]
                Byh = [NB("dyh%d" % i) for i in range(4)]
                memset("pool", qpad, 0.0, Bq)
                memset("pool", Vaug[:, :, :, 128:130], 1.0, [Bones])
                def d_proj(t):
                    w = t % 2
                    hT_load(t, hTt[w], BhTt[w])
                    yield
                    hT_, BhT_ = hTt[w], BhTt[w]
                    bk = (0, 1, 2) if (w == 0 or os.environ.get("BK")) else (3, 4, 5)
                    for i in range(3):
                        for k in range(8):
                            mm(ps[bk[i]][:, 0:256], hT_[:, k, :], Wd[:, k, i * 256:(i + 1) * 256], k == 0, k == 7,
                               [BhT_, BWd], PB[bk[i]])
                        yield
                    cs, sn = bc_in(cosd[:, t, :], 4), bc_in(sind[:, t, :], 4)
                    for (bank, dst, Bd) in ((bk[0], qr[w], Bqr[w]), (bk[1], kr[w], Bkr[w])):
                        cp("act", dst, ps[bank][:, 0:256], [PB[bank]], [Bd])
                        s3 = v3(ps[bank][:, 0:256], 64)
                        d3 = v3(dst, 64)
                        x1, x2 = s3[:, :, 0:8], s3[:, :, 8:16]
                        tt("dve", v3(ra[w], 8), x1, cs, ALU.mult, [PB[bank], Btab], [Bra[w]])
                        tt("dve", v3(rb[w], 8), x2, sn, ALU.mult, [PB[bank], Btab], [Brb[w]])
                        yield
                        tt("dve", v3(rc[w], 8), x2, cs, ALU.mult, [PB[bank], Btab], [Brc_[w]])
                        tt("dve", v3(rd[w], 8), x1, sn, ALU.mult, [PB[bank], Btab], [Brd[w]])
                        tt("dve", d3[:, :, 0:8], v3(ra[w], 8), v3(rb[w], 8), ALU.subtract, [Bra[w], Brb[w]], [Bd])
                        yield
                        tt("dve", d3[:, :, 8:16], v3(rc[w], 8), v3(rd[w], 8), ALU.add, [Brc_[w], Brd[w]], [Bd])
                    cp("act", Vaug[:, t, :, 0:128], v3(ps[bk[2]][:, 0:256], 128), [PB[bk[2]]], [Bv[t]])
                    yield
                    c0 = 0 if w == 0 else 4
                    for c in range(2):
                        tr(pv7[:, c0 + c, :], qr[w][:, c * 128:(c + 1) * 128], [Bqr[w]], PB[7])
                        tr(pv7[:, c0 + 2 + c, :], kr[w][:, c * 128:(c + 1) * 128], [Bkr[w]], PB[7])
                    yield
                    ts_ = slice(t * 128, (t + 1) * 128)
                    cp("dve", qpad[0:64, :, 0, ts_], pv7[0:64, c0:c0 + 2, :], [PB[7]], [Bq[t]])
                    cp("act", qpad[64:128, :, 1, ts_], pv7[64:128, c0:c0 + 2, :], [PB[7]], [Bq[t]])
                    cp("dve", kTd[:, :, ts_], pv7[:, c0 + 2:c0 + 4, :], [PB[7]], [Bk[t]])
                    yield

                interleave([d_proj(t) for t in range(NT)], int(os.environ.get("WD", "2")))
                its = []
                for G2 in range(8):
                    q0, q1 = 2 * G2, 2 * G2 + 1
                    for hl in range(2):
                        for kt in range(q1 + 1):
                            its.append((q0, q1, hl, kt, len(its)))

                SBK = [0, 1, 6]
                SAP = [ps[0][:], ps[1][:], ps[6][:].bitcast(F32)]
                LA = 2

                def d_S(it):
                    q0, q1, hl, kt, n = it
                    qs = max(kt, q0)
                    N = (q1 + 1 - qs) * 128
                    sbk = SBK[n % 3]
                    S = v3(SAP[n % 3], 256)
                    for j in range(2):
                        mm(S[:, j, 0:N], kTd[:, hl, kt * 128:(kt + 1) * 128], qpad[:, hl, j, qs * 128:(q1 + 1) * 128],
                           True, True, [Bk[kt]] + Bq[qs:q1 + 1], PB[sbk])

                deferred = []

                def d_rest(it):
                    q0, q1, hl, kt, n = it
                    h = 2 * hp + hl
                    qs = max(kt, q0)
                    N = (q1 + 1 - qs) * 128
                    sbk = SBK[n % 3]
                    pi = n % 4
                    S = v3(SAP[n % 3], 256)
                    Pv = v3(Pt[pi], 256)
                    act(Pv[:, :, 0:N], S[:, :, 0:N], ACT.Exp, [PB[sbk]], [BPt[pi]], scale=0.125)
                    if kt == qs:
                        memset("pool", Pv[64:128, :, 0:64], 0.0, [BPt[pi]])
                    for qi, qt in enumerate(range(qs, q1 + 1)):
                        obs = [2 + (qt % 2) * 2 + j for j in range(2)]
                        Oj2 = [ps[b][:, 0:129] for b in obs]
                        for j in range(2):
                            mm(Oj2[j], Pv[:, j, qi * 128:(qi + 1) * 128], Vaug[:, kt, hl, 0:129],
                               kt == 0, kt == qt, [BPt[pi], Bv[kt], Bones], PB[obs[j]])
                        if kt == qt:
                            fi = fin_i[0] % 4
                            fin_i[0] += 1
                            sl, sb_ = stat_slot()
                            for j in range(2):
                                recip(sl[:, j:j + 1], Oj2[j][:, 128:129], [PB[obs[j]]], [sb_])
                            tt("dve", sl[:, 2:3], sl[:, 1:2], neglam, ALU.mult, [sb_, Blsl], [sb_])
                            ts("dve", o1[fi], Oj2[1][:, 0:128], sl[:, 2:3], None, ALU.mult, None, [PB[obs[1]], sb_], [Bo1[fi]])
                            stt("dve", o2[fi], Oj2[0][:, 0:128], sl[:, 0:1], o1[fi], ALU.mult, ALU.add,
                                [PB[obs[0]], sb_, Bo1[fi]], [Bo2[fi]])

                            def finB(fi=fi):
                                so, sbo = stat_slot()
                                act(junk[:, 0:128], o2[fi], ACT.Square, [Bo2[fi]], [sbo], accum=so[:, 0:1])
                                act(so[:, 1:2], so[:, 0:1], ACT.Ln, [sbo, Beps], [sbo], scale=1.0 / 128, bias=epsT[:, 0:1])
                                act(so[:, 2:3], so[:, 1:2], ACT.Exp, [sbo], [sbo], scale=-0.5)
                                ts("dve", yh[fi], o2[fi], so[:, 2:3], 1.0 - lam_i, ALU.mult, ALU.mult, [Bo2[fi], sbo], [Byh[fi]])

                            def finC(fi=fi, h=h, qt=qt):
                                tr(pv7[:, 4 + fi, :], yh[fi], [Byh[fi]], PB[7])
                                cp("act", yT[:, 8 + h, qt * 128:(qt + 1) * 128], pv7[:, 4 + fi, :], [PB[7]], [ByT[2][qt]])
                                if dbg == "diff":
                                    dbgflag[0] = True
                                    dma("pool", out_d[qt * 128:(qt + 1) * 128, h * 128:(h + 1) * 128], yh[fi], [Byh[fi]],
                                        [NB("dbgout")])
                            deferred.append((n + 2, finB))
                            deferredC.append((n + 4, finC))

                deferredC = []
                fin_i = [0]
                for n0 in range(LA):
                    d_S(its[n0])
                for n, it in enumerate(its):
                    while deferredC and deferredC[0][0] <= n:
                        deferredC.pop(0)[1]()
                    while deferred and deferred[0][0] <= n:
                        deferred.pop(0)[1]()
                    if n + LA < len(its):
                        d_S(its[n + LA])
                    d_rest(it)
                while deferred:
                    deferred.pop(0)[1]()
                while deferredC:
                    deferredC.pop(0)[1]()
                P.barrier()
            A.off = mark
            if dbg == "diff":
                return

            qn, kn = A.alloc(256, F32), A.alloc(128, F32)
            Bqn, Bkn = NB("qn"), NB("kn")
            load_gbc(qn, W["mla_q_norm"][l, :], Bqn)
            load_gbc(kn, W["mla_kv_norm"][l, :], Bkn)
            cT = v3(A.alloc(3 * 2048), 2048)
            BcT = [NB("cT%d" % t) for t in range(NT)]
            krA = v3(A.alloc(NT * 32), 32)
            Bkra = [NB("kra%d" % t) for t in range(NT)]
            mark3 = A.off
            Wa = v3(A.alloc(8 * 416), 416)
            BWa = NB("Wa")
            wload(Wa, win[:, :, 1536:1952], BWa)
            cqn, ckn = [A.alloc(256) for _ in range(2)], [A.alloc(128) for _ in range(2)]
            Bcqn, Bckn = [NB("cqn0"), NB("cqn1")], [NB("ckn0"), NB("ckn1")]
            ra, rb = [A.alloc(64, F32) for _ in range(2)], [A.alloc(64, F32) for _ in range(2)]
            Bra, Brb = [NB("mra0"), NB("mra1")], [NB("mrb0"), NB("mrb1")]
            Bra2, Brb2 = [NB("mra20"), NB("mra21")], [NB("mrb20"), NB("mrb21")]

            def l_prep(t):
                w = t % 2
                hT_load(t, hTt[w], BhTt[w])
                yield
                hT_, BhT_ = hTt[w], BhTt[w]
                pa = ps[w]
                for k in range(8):
                    mm(pa[:, 0:416], hT_[:, k, :], Wa[:, k, :], k == 0, k == 7, [BhT_, BWa], PB[w])
                yield
                so, sbo = stat_slot()
                rstd_of(pa[:, 0:256], 256, [PB[w]], so[:, 0:1], sbo)
                so2, sbo2 = stat_slot()
                rstd_of(pa[:, 256:384], 128, [PB[w]], so2[:, 0:1], sbo2)
                yield
                x1, x2 = pa[:, 384:400], pa[:, 400:416]
                cs, sn = cosm[:, t, :], sinm[:, t, :]
                tt("dve", ra[w][:, 0:16], x1, cs, ALU.mult, [PB[w], Btab], [Bra[w]])
                tt("dve", rb[w][:, 0:16], x2, sn, ALU.mult, [PB[w], Btab], [Brb[w]])
                yield
                tt("dve", ra[w][:, 16:32], x2, cs, ALU.mult, [PB[w], Btab], [Bra2[w]])
                tt("dve", rb[w][:, 16:32], x1, sn, ALU.mult, [PB[w], Btab], [Brb2[w]])
                tt("dve", krA[:, t, 0:16], ra[w][:, 0:16], rb[w][:, 0:16], ALU.subtract, [Bra[w], Brb[w]], [Bkra[t]])
                yield
                tt("dve", krA[:, t, 16:32], ra[w][:, 16:32], rb[w][:, 16:32], ALU.add, [Bra2[w], Brb2[w]], [Bkra[t]])
                stt("dve", cqn[w], pa[:, 0:256], so[:, 0:1], qn, ALU.mult, ALU.mult, [PB[w], sbo, Bqn], [Bcqn[w]])
                stt("dve", ckn[w], pa[:, 256:384], so2[:, 0:1], kn, ALU.mult, ALU.mult, [PB[w], sbo2, Bkn], [Bckn[w]])
                yield
                c0 = 3 * w
                tr(pv7[:, c0 + 0, :], cqn[w][:, 0:128], [Bcqn[w]], PB[7])
                tr(pv7[:, c0 + 1, :], cqn[w][:, 128:256], [Bcqn[w]], PB[7])
                tr(pv7[:, c0 + 2, :], ckn[w], [Bckn[w]], PB[7])
                yield
                cp("act", cT[:, :, t * 128:(t + 1) * 128], pv7[:, c0:c0 + 3, :], [PB[7]], [BcT[t]])
                yield

            interleave([l_prep(t) for t in range(NT)], 2)
            P.barrier()
            sc_m = 96.0 ** -0.5
            for hg in range(2):
                A.off = mark3
                wqb = v3(A.alloc(2 * 384), 384)
                wkvb = A.alloc(512)
                Bwqb, Bwkvb = NB("wqb"), NB("wkvb")
                wload(wqb, kview(W["mla_wq_b"][l])[:, :, hg * 384:(hg + 1) * 384], Bwqb)
                wload(wkvb, W["mla_wkv_b"][l][:, hg * 512:(hg + 1) * 512], Bwkvb)
                qTm = v3(A.alloc(4 * 2048), 2048)
                kTm = v3(A.alloc(4 * 2048), 2048)
                Vm = A.alloc(NT * 4 * 66).rearrange("p (t h n) -> p t h n", t=NT, h=4)
                Bq = [NB("mq%d" % t) for t in range(NT)]
                Bk = [NB("mk%d" % t) for t in range(NT)]
                Bv = [NB("mv%d" % t) for t in range(NT)]
                Bones = NB("mones")
                Qtm, Ktm = [A.alloc(384) for _ in range(2)], [A.alloc(384) for _ in range(2)]
                BQtm = [NB("Qtm0"), NB("Qtm1")]
                BKtm1, BKtm2 = [NB("Ktm1a"), NB("Ktm1b")], [NB("Ktm2a"), NB("Ktm2b")]
                ra, rb = [A.alloc(64, F32) for _ in range(2)], [A.alloc(64, F32) for _ in range(2)]
                rc, rd = [A.alloc(64, F32) for _ in range(2)], [A.alloc(64, F32) for _ in range(2)]
                Brc_, Brd = [NB("mrc0"), NB("mrc1")], [NB("mrd0"), NB("mrd1")]
                Pt = [A.alloc(256) for _ in range(3)]
                BPt = [NB("mP%d" % i) for i in range(3)]
                ytm = [A.alloc(256) for _ in range(4)]
                Bytm = [NB("ytm%d" % i) for i in range(4)]
                memset("pool", Vm[:, :, :, 64:66], 1.0, [Bones])
                def l_proj(t):
                    w = t % 2
                    ts_ = slice(t * 128, (t + 1) * 128)
                    bq, bkv = (0, 1) if w == 0 else (2, 3)
                    pvw = pv7 if w == 0 else pv6
                    PBw = PB[7] if w == 0 else PB[6]
                    for kc in range(2):
                        mm(ps[bq][:, 0:384], cT[:, kc, ts_], wqb[:, kc, :], kc == 0, kc == 1, [BcT[t], Bwqb], PB[bq])
                    mm(ps[bkv][:], cT[:, 2, ts_], wkvb, True, True, [BcT[t], Bwkvb], PB[bkv])
                    yield
                    cp("act", Qtm[w], ps[bq][:, 0:384], [PB[bq]], [BQtm[w]])
                    s3, d3 = v3(ps[bq][:, 0:384], 96), v3(Qtm[w], 96)
                    x1, x2 = s3[:, :, 64:80], s3[:, :, 80:96]
                    cs, sn = bc_in(cosm[:, t, :], 4), bc_in(sinm[:, t, :], 4)
                    tt("dve", v3(ra[w], 16), x1, cs, ALU.mult, [PB[bq], Btab], [Bra[w]])
                    tt("dve", v3(rb[w], 16), x2, sn, ALU.mult, [PB[bq], Btab], [Brb[w]])
                    yield
                    tt("dve", v3(rc[w], 16), x2, cs, ALU.mult, [PB[bq], Btab], [Brc_[w]])
                    tt("dve", v3(rd[w], 16), x1, sn, ALU.mult, [PB[bq], Btab], [Brd[w]])
                    tt("dve", d3[:, :, 64:80], v3(ra[w], 16), v3(rb[w], 16), ALU.subtract, [Bra[w], Brb[w]], [BQtm[w]])
                    yield
                    tt("dve", d3[:, :, 80:96], v3(rc[w], 16), v3(rd[w], 16), ALU.add, [Brc_[w], Brd[w]], [BQtm[w]])
                    kv3 = v3(ps[bkv][:], 128)
                    k3 = v3(Ktm[w], 96)
                    cp("act", k3[:, :, 0:64], kv3[:, :, 0:64], [PB[bkv]], [BKtm1[w]])
                    cp("pool", k3[:, :, 64:96], bc_in(krA[:, t, :], 4), [Bkra[t]], [BKtm2[w]])
                    cp("dve", Vm[:, t, :, 0:64], kv3[:, :, 64:128], [PB[bkv]], [Bv[t]])
                    yield
                    for hl in range(4):
                        tr(pvw[0:96, hl, :], d3[:, hl, :], [BQtm[w]], PBw)
                        tr(pvw[0:96, 4 + hl, :], k3[:, hl, :], [BKtm1[w], BKtm2[w]], PBw)
                    cp("dve", qTm[0:96, :, ts_], pvw[0:96, 0:4, :], [PBw], [Bq[t]])
                    cp("act", kTm[0:96, :, ts_], pvw[0:96, 4:8, :], [PBw], [Bk[t]])
                    yield

                interleave([l_proj(t) for t in range(NT)], 2)
                its = []
                for G2 in range(8):
                    q0, q1 = 2 * G2, 2 * G2 + 1
                    for hl in range(4):
                        for kt in range(q1 + 1):
                            its.append((q0, q1, hl, kt, len(its)))

                SBK = [0, 1, 7]
                SAP = [ps[0][:], ps[1][:], ps[7][:].bitcast(F32)]
                LA = 2

                def m_S(it):
                    q0, q1, hl, kt, n = it
                    qs = max(kt, q0)
                    N = (q1 + 1 - qs) * 128
                    sbk = SBK[n % 3]
                    mm(SAP[n % 3][:, 0:N], kTm[0:96, hl, kt * 128:(kt + 1) * 128], qTm[0:96, hl, qs * 128:(q1 + 1) * 128],
                       True, True, [Bk[kt]] + Bq[qs:q1 + 1], PB[sbk])

                deferred = []

                def m_rest(it):
                    q0, q1, hl, kt, n = it
                    qs = max(kt, q0)
                    N = (q1 + 1 - qs) * 128
                    sbk = SBK[n % 3]
                    pi = n % 3
                    act(Pt[pi][:, 0:N], SAP[n % 3][:, 0:N], ACT.Exp, [PB[sbk]], [BPt[pi]], scale=sc_m)
                    if kt == qs:
                        memset("pool", Pt[pi][64:128, 0:64], 0.0, [BPt[pi]])
                    for qi, qt in enumerate(range(qs, q1 + 1)):
                        ob = 2 + (qt % 2) + 2 * (hl % 2)
                        O = ps[ob][:, 0:65]
                        mm(O, Pt[pi][:, qi * 128:(qi + 1) * 128], Vm[:, kt, hl, 0:65], kt == 0, kt == qt,
                           [BPt[pi], Bv[kt], Bones], PB[ob])
                        if kt == qt:
                            G2 = q0 // 2
                            yi = 2 * (G2 % 2) + (qt % 2)
                            sl, sb_ = stat_slot()
                            recip(sl[:, 0:1], O[:, 64:65], [PB[ob]], [sb_])
                            ts("dve", ytm[yi][:, hl * 64:(hl + 1) * 64], O[:, 0:64], sl[:, 0:1], None, ALU.mult, None,
                               [PB[ob], sb_], [Bytm[yi]])
                            if hl == 3:
                                def fin(qt=qt, yi=yi):
                                    y2 = ytm[yi]
                                    for c in range(2):
                                        tr(pv6[:, 2 * (qt % 2) + c, :], y2[:, c * 128:(c + 1) * 128], [Bytm[yi]], PB[6])
                                    cp("dve", yT[:, 4 + 2 * hg:6 + 2 * hg, qt * 128:(qt + 1) * 128],
                                       pv6[:, 2 * (qt % 2):2 * (qt % 2) + 2, :], [PB[6]], [ByT[1][qt]])
                                    if dbg == "mla":
                                        dbgflag[0] = True
                                        dma("pool", out_d[qt * 128:(qt + 1) * 128, hg * 256:(hg + 1) * 256], y2, [Bytm[yi]],
                                            [NB("dbgout")])
                                deferred.append((n + 3, fin))

                for n0 in range(LA):
                    m_S(its[n0])
                for n, it in enumerate(its):
                    while deferred and deferred[0][0] <= n:
                        deferred.pop(0)[1]()
                    if n + LA < len(its):
                        m_S(its[n + LA])
                    m_rest(it)
                while deferred:
                    deferred.pop(0)[1]()
                P.barrier()
            A.off = mark
            if dbg == "mla":
                return

            g2 = A.alloc(D, F32)
            Bg2 = NB("g2")
            load_gbc(g2, mn[l, 1, :], Bg2)
            wout = v3(A.alloc(8 * 1024), 1024)
            Bwout = NB("wout")
            wload(wout, kview(W["w_out"][l]), Bwout)
            hTg = v3(A.alloc(8 * 512), 512)
            BhTg = [NB("hTg%d" % i) for i in range(4)]
            mfc = [A.alloc(512, F32) for _ in range(2)]
            Bmfc = [NB("mfc0"), NB("mfc1")]
            mb = v3(A.alloc(8 * 512), 512)
            Bmb = [NB("mb%d" % c) for c in range(8)]
            wg = [A.alloc(3 * 8 * 128).rearrange("p (b k n) -> p b k n", b=3, k=8) for _ in range(2)]
            wbr = [A.alloc(3 * 4 * 128).rearrange("p (b k n) -> p b k n", b=3, k=4) for _ in range(2)]
            Bwg = [NB("wg0"), NB("wg1")]
            sig = [A.alloc(512, F32) for _ in range(2)]
            Bsig = [NB("sig0"), NB("sig1")]
            tmpm = A.alloc(512, F32)
            Btmpm = NB("tmpm")
            cw = 0
            cs_ = 0
            for g in range(4):
                gs = slice(g * 512, (g + 1) * 512)
                for i in range(4):
                    t = 4 * g + i
                    hT_load(t, hTg[:, :, i * 128:(i + 1) * 128], BhTg[i])
                for c in range(8):
                    wi = cw % 2
                    cw += 1
                    P.add("pool", lambda e, wi=wi, c=c: [
                        e.dma_start(out=wg[wi].rearrange("p b k n -> p (b k n)"), in_=W["wgate"][l, c]),
                        e.dma_start(out=wbr[wi].rearrange("p b k n -> p (b k n)"), in_=W["wbr2"][l, c])],
                        writes=[Bwg[wi]], dma=True, ndma=2)
                    mi = c % 2
                    for b in range(3):
                        bg = 2 * b
                        for k in range(8):
                            mm(ps[bg][:], wg[wi][:, b, k, :], hTg[:, k, :], k == 0, k == 7, [Bwg[wi]] + BhTg, PB[bg])
                        for k in range(4):
                            mm(ps[bg + 1][:], wbr[wi][:, b, k, :], yT[:, 4 * b + k, gs], k == 0, k == 3,
                               [Bwg[wi]] + ByT[b][4 * g:4 * g + 4], PB[bg + 1])
                        si = cs_ % 2
                        cs_ += 1
                        act(sig[si], ps[bg][:], ACT.Sigmoid, [PB[bg]], [Bsig[si]])
                        if b == 0:
                            tt("dve", mfc[mi], sig[si], ps[bg + 1][:], ALU.mult, [Bsig[si], PB[bg + 1]], [Bmfc[mi]])
                        else:
                            tt("dve", tmpm, sig[si], ps[bg + 1][:], ALU.mult, [Bsig[si], PB[bg + 1]], [Btmpm])
                            tt("dve", mfc[mi], mfc[mi], tmpm, ALU.add, [Bmfc[mi], Btmpm], [Bmfc[mi]])
                    cp("act", mb[:, c, :], mfc[mi], [Bmfc[mi]], [Bmb[c]])
                for i in range(4):
                    t = 4 * g + i
                    yb_ = (0, 1) if i % 2 == 0 else (2, 3)
                    for oh in range(2):
                        for c in range(8):
                            mm(ps[yb_[oh]][:], mb[:, c, i * 128:(i + 1) * 128], wout[:, c, oh * 512:(oh + 1) * 512],
                               c == 0, c == 7, [Bmb[c], Bwout], PB[yb_[oh]])
                    postnorm_add(t, yb_, g2, Bg2, False)
            P.barrier()

        if phases is None:
            phases_ = []
            for l in range(NL):
                phases_ += [(l, "ffn1"), (l, "mix"), (l, "cross"), (l, "ffn2")]
        else:
            phases_ = phases
        for (l, ph) in phases_:
            if ph == "ffn1":
                ffn(l, W["ffn1_w13"], W["ffn1_w2"], W["ffn1_norms"])
            elif ph == "ffn2":
                ffn(l, W["ffn2_w13"], W["ffn2_w2"], W["ffn2_norms"])
            elif ph == "cross":
                cross(l)
            elif ph.startswith("mix"):
                mixer(l, dbg=ph[4:] if len(ph) > 3 else None)

        Bout = NB("dbgout") if dbgflag[0] else Buf("out")
        for t in ([] if dbgflag[0] else range(NT)):
            P.add("sp", lambda e, t=t: e.dma_start(out=out_d[t * 128:(t + 1) * 128, :], in_=X[:, t, :]),
                  reads=[XB[t]], writes=[Bout], dma=True)
        P.add("sp", lambda e: e.nop(), reads=[Bout], real=False)
        P.emit()
    return nc


_CACHE = {}


def make_in_maps(inputs, cores):
    inv, rc = host_consts()
    maps = []
    shared = {}
    for k, v in inputs.items():
        if k in ("x", "mem", "positions"):
            continue
        a = np.ascontiguousarray(v)
        if k == "diff_lambda":
            a = a.reshape(4, 256)
        if k == "w_in":
            gcols = a[:, :, 3488:].reshape(4, 8, 128, 3, 8, 128)
            shared["wgate"] = np.ascontiguousarray(gcols.transpose(0, 4, 2, 3, 1, 5)).reshape(4, 8, 128, 3072)
            a = np.ascontiguousarray(a[:, :, 0:3488])
        if k == "w_branch":
            wb = a.reshape(4, 3, 4, 128, 8, 128)
            shared["wbr2"] = np.ascontiguousarray(wb.transpose(0, 4, 3, 1, 2, 5)).reshape(4, 8, 128, 1536)
            continue
        shared[k] = a
    for b in cores:
        m = dict(shared)
        m["x"] = np.ascontiguousarray(inputs["x"][b])
        m["mem"] = np.ascontiguousarray(inputs["mem"][b])
        m["pos"] = np.ascontiguousarray(inputs["positions"][b].reshape(NT, 128).T.astype(np.int32))
        m["inv"] = inv
        m["rc"] = rc
        maps.append(m)
    return maps


def kernel(**inputs):
    if "nc" not in _CACHE:
        _CACHE["nc"] = build_program()
    nc = _CACHE["nc"]
    maps = make_in_maps(inputs, list(range(8)))
    res = run_bass_kernel_spmd(nc, maps, core_ids=list(range(8)))
    out = np.stack([np.asarray(r["out"]) for r in res.results], axis=0)
    return out.astype(np.float32)
```

```python
import math
import os
FSTAGE = int(os.environ.get('FSTAGE', '9'))
PSTAGE = int(os.environ.get('PSTAGE', '9'))
NTILE = int(os.environ.get('NTILE', '8'))
NOROPE = int(os.environ.get('NOROPE', '0'))
RSTAGE = int(os.environ.get('RSTAGE', '9'))
from contextlib import ExitStack
import numpy as np
import concourse.bass as bass
import concourse.mybir as mybir
from concourse.bass_utils import run_bass_kernel_spmd

ACT = mybir.ActivationFunctionType
ALU = mybir.AluOpType
AX = mybir.AxisListType
F32, BF16, I32 = mybir.dt.float32, mybir.dt.bfloat16, mybir.dt.int32

ENGS = ("pe", "act", "dve", "pool", "sp")
EIDX = {e: i for i, e in enumerate(ENGS)}
NE = len(ENGS)
EPS = 1e-6
PI = math.pi

S, D, NT = 2048, 1024, 16
DFF = 2816
NF = 22
INW = 6560


class Buf:
    __slots__ = ("name", "wE", "wD", "rE", "rD", "chan", "excl")

    def __init__(self, name, excl=False):
        self.name = name
        self.excl = excl
        self.wE = {}
        self.wD = []
        self.rE = {}
        self.rD = []
        self.chan = None


_BUFS = {}


def NB(name):
    b = _BUFS.get(name)
    if b is None:
        b = _BUFS[name] = Buf(name)
    return b


class Op:
    __slots__ = ("eng", "fn", "waits", "dwaits", "signal", "pos", "clock", "dma_chan", "ticket", "real")


class Prog:
    def __init__(self, nc):
        self.nc = nc
        self.ops = {e: [] for e in ENGS}
        self.known = {e: [-1] * NE for e in ENGS}
        self.dknown = {e: {} for e in ENGS}
        self.chans = {}
        self.nbuf = 0

    def _needE(self, eng, e2, p, op, same_ok):
        if e2 == eng and same_ok:
            return
        k = self.known[eng]
        i2 = EIDX[e2]
        if k[i2] >= p:
            return
        op.waits.append((e2, p))
        src = self.ops[e2][p]
        src.signal = True
        ck = src.clock
        for i in range(NE):
            if ck[i] > k[i]:
                k[i] = ck[i]
        k[i2] = p

    def _needD(self, eng, ch, v, op):
        dk = self.dknown[eng]
        if dk.get(ch, 0) >= v:
            return
        op.dwaits.append((ch, v))
        dk[ch] = v

    def add(self, eng, fn, reads=(), writes=(), dma=False, ndma=1, pe_acc=False, real=True, extra=()):
        op = Op()
        op.real = real
        op.eng = eng
        op.fn = fn
        op.waits = []
        op.dwaits = []
        op.signal = False
        op.pos = len(self.ops[eng])
        op.dma_chan = None
        op.ticket = None
        for e2, p in extra:
            self._needE(eng, e2, p, op, True)
        for b in reads:
            for e2, p in b.wE.items():
                self._needE(eng, e2, p, op, False)
            for ch, v in b.wD:
                self._needD(eng, ch, v, op)
            if b.excl:
                for e2, p in b.rE.items():
                    self._needE(eng, e2, p, op, True)
        for b in writes:
            for e2, p in b.wE.items():
                self._needE(eng, e2, p, op, pe_acc)
            for ch, v in b.wD:
                self._needD(eng, ch, v, op)
            for e2, p in b.rE.items():
                self._needE(eng, e2, p, op, eng == "pe")
            for ch, v in b.rD:
                self._needD(eng, ch, v, op)
        if dma:
            cb = writes[0] if writes else reads[0]
            if cb.chan is None:
                cb.chan = "c%d" % len(self.chans)
                self.chans[cb.chan] = 0
            self.chans[cb.chan] += 16 * ndma
            op.dma_chan = cb.chan
            ev = (cb.chan, self.chans[cb.chan])
        op.clock = list(self.known[eng])
        self.ops[eng].append(op)
        for b in writes:
            b.rE = {}
            b.rD = []
            if dma:
                b.wE = {}
                b.wD = [ev]
            else:
                if pe_acc and not b.wD and list(b.wE.keys()) == ["pe"]:
                    b.wE["pe"] = op.pos
                else:
                    b.wE = {eng: op.pos}
                    b.wD = []
        for b in reads:
            if b in writes:
                continue
            if dma:
                b.rD.append(ev)
            else:
                b.rE[eng] = op.pos

    def barrier(self):
        tg = []
        for e in ENGS:
            if e == "sp":
                continue
            for p in range(len(self.ops[e]) - 1, -1, -1):
                o = self.ops[e][p]
                if o.real and o.dma_chan is None:
                    tg.append((e, p))
                    break
        for e in ENGS:
            self.add(e, lambda en: en.nop(), real=False, extra=tg)

    def emit(self):
        nc = self.nc
        for e in ENGS:
            t = 0
            for op in self.ops[e]:
                if op.signal:
                    t += 1
                    op.ticket = t
        with ExitStack() as st:
            esem = {e: st.enter_context(nc.semaphore("s_" + e)) for e in ENGS if e != "sp"}
            csem = {c: st.enter_context(nc.semaphore(c)) for c in self.chans}
            allsems = list(esem.values()) + list(csem.values())
            with nc.Block() as pre:
                def clr(eng):
                    for sm in allsems:
                        eng.sem_clear(sm)
                pre.gpsimd(clr)
            block = st.enter_context(nc.Block())
            ops = self.ops

            def run(e, engobj):
                for op in ops[e]:
                    for (e2, p) in op.waits:
                        engobj.wait_ge(esem[e2], ops[e2][p].ticket)
                    for (ch, v) in op.dwaits:
                        engobj.wait_ge(csem[ch], v)
                    r = op.fn(engobj)
                    if op.dma_chan is not None:
                        if not isinstance(r, (list, tuple)):
                            r = [r]
                        for ins in r:
                            ins.then_inc(csem[op.dma_chan], 16)
                    elif op.signal:
                        r.then_inc(esem[e], 1)

            block.tensor(lambda eng: run("pe", eng))
            block.scalar(lambda eng: run("act", eng))
            block.vector(lambda eng: run("dve", eng))
            block.gpsimd(lambda eng: run("pool", eng))
            block.sync(lambda eng: run("sp", eng))


class Arena:
    def __init__(self, ap, nelem):
        self.ap = ap
        self.n = nelem
        self.off = 0

    def reset(self):
        self.off = 0

    def alloc(self, nelem, dt=BF16):
        n2 = nelem * (2 if dt == F32 else 1)
        n2 = (n2 + 31) // 32 * 32
        assert self.off + n2 <= self.n, ("arena overflow", self.off, n2, self.n)
        v = self.ap[:, self.off:self.off + n2]
        self.off += n2
        if dt == F32:
            v = v.bitcast(F32)
        return v[:, 0:nelem]


def bc_mid(ap2, n):
    return ap2.unsqueeze(2).to_broadcast([ap2.shape[0], ap2.shape[1], n])


def bc_in(ap2, a):
    return ap2.unsqueeze(1).to_broadcast([ap2.shape[0], a, ap2.shape[1]])


def host_consts():
    inv_ret = 1.0 / (np.float32(10000.0) ** (np.arange(0, 64, 2, dtype=np.float32) / np.float32(64)))
    inv_mla = 1.0 / (np.float32(500000.0) ** (np.arange(0, 32, 2, dtype=np.float32) / np.float32(32)))
    inv_dif = 1.0 / (np.float32(500000.0) ** (np.arange(0, 16, 2, dtype=np.float32) / np.float32(16)))
    inv = np.concatenate([inv_ret, inv_mla, inv_dif]).astype(np.float32)
    inv = np.tile(inv[None, :], (128, 1))
    gam = 1.0 - 2.0 ** (-5.0 - np.arange(4, dtype=np.float64))
    lg = np.log(gam)
    n = np.arange(128)
    dm = np.zeros((128, 4, 128), np.float64)
    for h in range(4):
        nn = n[None, :]
        mm = n[:, None]
        same = (nn // 64) == (mm // 64)
        later = (nn // 64) > (mm // 64)
        d_ = np.where(same, np.exp(lg[h] * np.abs(nn - mm)), np.where(later, np.exp(lg[h] * (nn - mm)), 0.0))
        dm[:, h, :] = d_ * 0.125
    qd = np.zeros((128, 2, 128), np.float64)
    for c in range(2):
        for j in range(2):
            qd[j * 64:(j + 1) * 64, c, :] = 0.125 * np.exp(lg[2 * c + j] * (n[None, :] + 1.0))
    kd = np.zeros((128, 4), np.float64)
    for h in range(4):
        kd[:, h] = np.exp(lg[h] * (127.0 - n))
    cd = np.zeros((128, 2), np.float64)
    for c in range(2):
        for j in range(2):
            cd[j * 64:(j + 1) * 64, c] = np.exp(lg[2 * c + j] * 128.0)
    rc = np.concatenate([dm.reshape(128, 512), qd.reshape(128, 256), kd, cd], axis=1).astype(np.float32)
    return inv, rc


RC_COLS = 512 + 256 + 4 + 2


def build_program(NL=4, stop=None, phases=None):
    _BUFS.clear()
    nc = bass.Bass("TRN2", target_bir_lowering=False)

    def din(name, shape, dt=F32):
        return nc.dram_tensor(name, list(shape), dt, kind="ExternalInput").ap()

    x_d = din("x", [S, D])
    mem_d = din("mem", [256, D])
    pos_d = din("pos", [128, NT], I32)
    inv_d = din("inv", [128, 56])
    rc_d = din("rc", [128, RC_COLS])
    W = {}
    for nm, shp in [("ffn1_norms", [4, 2, D]), ("ffn1_w13", [4, D, 2 * DFF]), ("ffn1_w2", [4, DFF, D]),
                    ("mix_norms", [4, 2, D]), ("w_in", [4, D, 3488]), ("wgate", [4, 8, 128, 3072]), ("wbr2", [4, 8, 128, 1536]), ("mla_q_norm", [4, 256]),
                    ("mla_kv_norm", [4, 128]), ("mla_wq_b", [4, 256, 768]), ("mla_wkv_b", [4, 128, 1024]),
                    ("diff_lambda", [4, 256]), ("w_out", [4, D, D]),
                    ("cross_norms", [4, 3, D]), ("cross_wq", [4, D, D]), ("cross_wkv", [4, D, 2 * D]),
                    ("cross_wo", [4, D, D]), ("ffn2_norms", [4, 2, D]), ("ffn2_w13", [4, D, 2 * DFF]),
                    ("ffn2_w2", [4, DFF, D])]:
        W[nm] = din(nm, shp)
    out_d = nc.dram_tensor("out", [S, D], F32, kind="ExternalOutput").ap()
    hTd = nc.dram_tensor("hT_scratch", [NT, 128, 8 * 128], BF16).ap()
    yTd = nc.dram_tensor("yT_scratch", [3, 128, 4 * 2048], BF16).ap()

    P = Prog(nc)
    st = ExitStack()
    with st:
        def sb(name, shape, dt):
            return st.enter_context(nc.sbuf_tensor(name, shape, dt))

        X = sb("X", [128, NT, D], F32)
        XB = [Buf("X%d" % t) for t in range(NT)]
        ident = sb("ident", [128, 128], BF16)
        Bident = Buf("ident")
        cosA = sb("cosA", [128, NT, 56], F32)
        sinA = sb("sinA", [128, NT, 56], F32)
        Btab = Buf("tab")
        rcs = sb("rcs", [128, RC_COLS], F32)
        Brc = Buf("rc")
        rstd_c = sb("rstd_c", [128, NT], F32)
        Brstd = [Buf("rstd%d" % t) for t in range(NT)]
        ARN = 65 * 1024
        arena_t = sb("arena", [128, ARN], BF16)
        A = Arena(arena_t, ARN)
        ps = [st.enter_context(nc.psum_tensor("ps%d" % i, [128, 512], F32)) for i in range(6)]
        ps += [st.enter_context(nc.psum_tensor("ps%d" % i, [128, 1024], BF16)) for i in (6, 7)]
        PB = [Buf("ps%d" % i, excl=True) for i in range(8)]

        def psb(i):
            return ps[i][:] if i >= 6 else ps[i][:].bitcast(BF16)

        def mm(out, lhsT, rhs, start, stop_, reads, wb):
            P.add("pe", lambda e: e.matmul(out, lhsT=lhsT, rhs=rhs, start=start, stop=stop_),
                  reads=reads, writes=[wb], pe_acc=True)

        def tr(out, in_, reads, wb):
            P.add("pe", lambda e: e.transpose(out=out, in_=in_, identity=ident[0:in_.shape[0], 0:in_.shape[0]]),
                  reads=list(reads) + [Bident], writes=[wb], pe_acc=True)

        def act(out, in_, func, reads, writes, scale=None, bias=None, accum=None):
            kw = {}
            if scale is not None:
                kw["scale"] = scale
            if bias is not None:
                kw["bias"] = bias
            if accum is not None:
                kw["accum_out"] = accum
            P.add("act", lambda e: e.activation(out=out, in_=in_, func=func, **kw), reads=reads, writes=writes)

        def tt(eng, out, in0, in1, op, reads, writes):
            P.add(eng, lambda e: e.tensor_tensor(out=out, in0=in0, in1=in1, op=op), reads=reads, writes=writes)

        def ts(eng, out, in0, s1, s2, op0, op1, reads, writes):
            if op1 is None:
                s2, op1 = 0.0, ALU.add
            P.add(eng, lambda e: e.tensor_scalar(out=out, in0=in0, scalar1=s1, scalar2=s2, op0=op0, op1=op1),
                  reads=reads, writes=writes)

        def stt(eng, out, in0, scalar, in1, op0, op1, reads, writes):
            P.add(eng, lambda e: e.scalar_tensor_tensor(out=out, in0=in0, scalar=scalar, in1=in1, op0=op0, op1=op1),
                  reads=reads, writes=writes)

        def cp(eng, out, in_, reads, writes):
            if eng == "act":
                P.add("act", lambda e: e.copy(out=out, in_=in_), reads=reads, writes=writes)
            else:
                P.add(eng, lambda e: e.tensor_copy(out=out, in_=in_), reads=reads, writes=writes)

        def recip(out, in_, reads, writes):
            P.add("dve", lambda e: e.reciprocal(out=out, in_=in_), reads=reads, writes=writes)

        def memset(eng, ap, val, writes):
            P.add(eng, lambda e: e.memset(ap, val), writes=writes)

        def dma(eng, out, in_, reads, writes):
            P.add(eng, lambda e: e.dma_start(out=out, in_=in_), reads=reads, writes=writes, dma=True)

        def wload(out, in_, wb):
            dma("pool", out, in_, [], [wb])

        def kview(w2d):
            return w2d.rearrange("(k p) n -> p k n", p=128)

        stat = sb("stat", [128, 64], F32)
        Bstat = [Buf("stat%d" % i) for i in range(16)]
        stat_i = [0]

        def stat_slot():
            i = stat_i[0] % 16
            stat_i[0] += 1
            return stat[:, 4 * i:4 * i + 4], Bstat[i]

        epsT = sb("epsT", [128, 1], F32)
        Beps = Buf("eps")
        memset("pool", epsT[:], EPS, [Beps])
        junk = sb("junk", [128, D], BF16)
        Bjunk = Buf("junk")

        def rstd_of(src, n, reads, out_ap, out_buf, ncols=1):
            sl, sbuf_ = stat_slot()
            act(junk[:, 0:n], src, ACT.Square, reads, [sbuf_], accum=sl[:, 0:1])
            act(sl[:, 1:2], sl[:, 0:1], ACT.Ln, [sbuf_, Beps], [sbuf_], scale=1.0 / n, bias=epsT[:, 0:1])
            act(out_ap, sl[:, 1:2], ACT.Exp, [sbuf_], [out_buf], scale=-0.5)

        memset("pool", ident[:], 0.0, [Bident])
        P.add("pool", lambda e: e.affine_select(out=ident[:], in_=ident[:], pattern=[[-1, 128]],
                                               compare_op=ALU.not_equal, fill=1.0, base=0, channel_multiplier=1),
              reads=[Bident], writes=[Bident])
        for t in range(NT):
            dma("sp", X[:, t, :], x_d[t * 128:(t + 1) * 128, :], [], [XB[t]])
        dma("sp", rcs[:], rc_d, [], [Brc])
        A.reset()
        posi = sb("posi", [128, NT], I32)
        posf = sb("posf", [128, NT], F32)
        invs = sb("invs", [128, 56], F32)
        ang = A.alloc(NT * 56, F32).rearrange("p (t i) -> p t i", i=56)
        tmpa = A.alloc(NT * 56, F32).rearrange("p (t i) -> p t i", i=56)
        Bpos, Binv, Bang, Btmpa = Buf("pos"), Buf("inv"), Buf("ang"), Buf("tmpa")
        dma("sp", posi[:], pos_d, [], [Bpos])
        dma("sp", invs[:], inv_d, [], [Binv])
        cp("dve", posf[:], posi[:], [Bpos], [Bpos])
        tt("dve", ang, bc_mid(posf[:], 56), bc_in(invs[:], NT), ALU.mult, [Bpos, Binv], [Bang])
        tmpi = A.alloc(NT * 56, F32).bitcast(I32).rearrange("p (t i) -> p t i", i=56)
        tmpf = A.alloc(NT * 56, F32).rearrange("p (t i) -> p t i", i=56)
        tmpg = A.alloc(NT * 56, F32).rearrange("p (t i) -> p t i", i=56)
        Btmpi, Btmpf, Btmpg = Buf("tmpi"), Buf("tmpf"), Buf("tmpg")
        for tab, ph in (() if NOROPE else ((sinA, 0.0), (cosA, 0.25))):
            ts("dve", tmpa, ang, 1.0 / (2.0 * PI), ph, ALU.mult, ALU.add, [Bang], [Btmpa])
            cp("dve", tmpi, tmpa, [Btmpa], [Btmpi])
            cp("dve", tmpf, tmpi, [Btmpi], [Btmpf])
            tt("dve", tmpa, tmpa, tmpf, ALU.subtract, [Btmpa, Btmpf], [Btmpa])
            ts("dve", tmpg, tmpa, 0.5, 1.0, ALU.is_gt, ALU.mult, [Btmpa], [Btmpg])
            tt("dve", tmpa, tmpa, tmpg, ALU.subtract, [Btmpa, Btmpg], [Btmpa])
            ts("dve", tmpg, tmpa, -0.5, 1.0, ALU.is_lt, ALU.mult, [Btmpa], [Btmpg])
            tt("dve", tmpa, tmpa, tmpg, ALU.add, [Btmpa, Btmpg], [Btmpa])
            ts("dve", tmpa, tmpa, 0.4999999, -0.4999999, ALU.min, ALU.max, [Btmpa], [Btmpa])
            act(tab[:], tmpa, ACT.Sin, [Btmpa], [Btab], scale=2.0 * PI)
        cosr, sinr = cosA[:, :, 0:32], sinA[:, :, 0:32]
        cosm, sinm = cosA[:, :, 32:48], sinA[:, :, 32:48]
        cosd, sind = cosA[:, :, 48:56], sinA[:, :, 48:56]
        dmaskT = rcs[:, 0:512].rearrange("p (h n) -> p h n", n=128)
        qdecT = rcs[:, 512:768].rearrange("p (c n) -> p c n", n=128)
        kdec = rcs[:, 768:772]
        cdcol = rcs[:, 772:774]
        P.barrier()

        def load_gbc(dst, vec_ap, buf):
            dma("sp", dst, vec_ap.partition_broadcast(128), [], [buf])

        def prenorm_tile(t, gbc, Bg, hb, Bhb, hT_out, BhT, pbank, cache=True, eng_evac="dve"):
            if not cache:
                rstd_of(X[:, t, :], D, [XB[t]], rstd_c[:, t:t + 1], Brstd[t])
            if PSTAGE < 2:
                return
            stt("dve", hb, X[:, t, :], rstd_c[:, t:t + 1], gbc, ALU.mult, ALU.mult, [XB[t], Brstd[t], Bg], [Bhb])
            if PSTAGE < 3:
                cp("dve", X[:, t, :], hb, [Bhb], [XB[t]])
                return
            pv = psb(pbank).rearrange("p (k n) -> p k n", n=128)
            for k in range(8):
                tr(pv[:, k, :], hb[:, k * 128:(k + 1) * 128], [Bhb], PB[pbank])
            if PSTAGE != 4:
                cp(eng_evac, hT_out, pv, [PB[pbank]], [BhT])

        def interleave(gens, W):
            gens = list(gens)
            active = []
            while gens or active:
                while gens and len(active) < W:
                    active.append(gens.pop(0))
                for g_ in list(active):
                    try:
                        next(g_)
                    except StopIteration:
                        active.remove(g_)

        def prenorm_gen(t, gbc, Bg, hb, Bhb, hT_out, BhT, pbank, cache=True, eng_evac="dve"):
            if not cache:
                rstd_of(X[:, t, :], D, [XB[t]], rstd_c[:, t:t + 1], Brstd[t])
                yield
            stt("dve", hb, X[:, t, :], rstd_c[:, t:t + 1], gbc, ALU.mult, ALU.mult, [XB[t], Brstd[t], Bg], [Bhb])
            yield
            pv = psb(pbank).rearrange("p (k n) -> p k n", n=128)
            for k in range(8):
                tr(pv[:, k, :], hb[:, k * 128:(k + 1) * 128], [Bhb], PB[pbank])
            cp(eng_evac, hT_out, pv, [PB[pbank]], [BhT])
            yield

        def postnorm_add(t, ybanks, gbc, Bg, half):
            sl, sbuf_ = stat_slot()
            for i, b in enumerate(ybanks):
                act(junk[:, i * 512:(i + 1) * 512], ps[b][:], ACT.Square, [PB[b]], [sbuf_], accum=sl[:, i:i + 1])
            tt("dve", sl[:, 2:3], sl[:, 0:1], sl[:, 1:2], ALU.add, [sbuf_], [sbuf_])
            act(sl[:, 3:4], sl[:, 2:3], ACT.Ln, [sbuf_, Beps], [sbuf_], scale=1.0 / D, bias=epsT[:, 0:1])
            act(sl[:, 0:1], sl[:, 3:4], ACT.Exp, [sbuf_], [sbuf_], scale=-0.5)
            for i, b in enumerate(ybanks):
                stt("dve", ps[b][:], ps[b][:], sl[:, 0:1], gbc[:, i * 512:(i + 1) * 512], ALU.mult, ALU.mult,
                    [sbuf_, Bg], [PB[b]])
                stt("dve", X[:, t, i * 512:(i + 1) * 512], ps[b][:], 0.5 if half else 1.0,
                    X[:, t, i * 512:(i + 1) * 512], ALU.mult, ALU.add, [PB[b]], [XB[t]])

        def ffn(l, w13, w2, norms):
            for hh in range(2):
                A.reset()
                hb = [A.alloc(D) for _ in range(4)]
                Bhb = [NB("hb%d" % i) for i in range(4)]
                sg = [A.alloc(512) for _ in range(2)]
                Bsg = [NB("sg%d" % i) for i in range(2)]
                g1 = A.alloc(D, F32)
                g2 = A.alloc(D, F32)
                Bg1, Bg2 = NB("g1"), NB("g2")
                load_gbc(g1, norms[l, 0, :], Bg1)
                load_gbc(g2, norms[l, 1, :], Bg2)
                a0 = A.off
                hT = A.alloc(8 * 1024).rearrange("p (k n) -> p k n", n=1024)
                BhT = [NB("hT%d" % i) for i in range(8)]
                NWB = 3
                wgu = [A.alloc(2 * 8 * 256).rearrange("p (s k n) -> p s k n", s=2, k=8) for _ in range(NWB)]
                Bwgu = [NB("wgu%d" % i) for i in range(NWB)]
                w2hi = arena_t[:, a0:a0 + 11 * 1024].rearrange("p (f n) -> p f n", n=1024)
                w2lo = A.alloc(11 * 1024).rearrange("p (f n) -> p f n", n=1024)
                w2p = (w2lo, w2hi)
                actT = A.alloc(NF * 1024).rearrange("p (f n) -> p f n", n=1024)
                BaT = [[NB("aT%d_%d" % (f, g)) for g in range(2)] for f in range(NF)]
                Bw2 = [NB("w2_%d" % i) for i in range(2)]
                interleave([prenorm_gen(hh * 8 + i, g1, Bg1, hb[i % 4], Bhb[i % 4], hT[:, :, i * 128:(i + 1) * 128],
                                        BhT[i], pbank=(6, 7, 0, 1)[i % 4], cache=False, eng_evac="act" if i % 2 else "dve")
                            for i in range(NTILE)], 4)
                if FSTAGE < 2:
                    P.barrier()
                    continue
                w2v = w2[l].rearrange("(f p) n -> p f n", p=128)
                wload(w2lo, w2v[:, 0:11, :], Bw2[0])
                w13v = kview(w13[l])
                cnt = 0
                for fg in range(11):
                    wb = fg % NWB
                    P.add("pool", lambda e, wb=wb, fg=fg: [
                        e.dma_start(out=wgu[wb][:, 0], in_=w13v[:, :, fg * 256:(fg + 1) * 256]),
                        e.dma_start(out=wgu[wb][:, 1], in_=w13v[:, :, DFF + fg * 256:DFF + (fg + 1) * 256])],
                        writes=[Bwgu[wb]], dma=True, ndma=2)
                    for fi in range(2):
                        f = fg * 2 + fi
                        for g in range(2):
                            bg = (cnt % 3) * 2
                            bu = bg + 1
                            cnt += 1
                            for k in range(8):
                                mm(ps[bg][:], wgu[wb][:, 0, k, fi * 128:(fi + 1) * 128], hT[:, k, g * 512:(g + 1) * 512],
                                   k == 0, k == 7, [Bwgu[wb]] + BhT[g * 4:(g + 1) * 4], PB[bg])
                            for k in range(8):
                                mm(ps[bu][:], wgu[wb][:, 1, k, fi * 128:(fi + 1) * 128], hT[:, k, g * 512:(g + 1) * 512],
                                   k == 0, k == 7, [Bwgu[wb]] + BhT[g * 4:(g + 1) * 4], PB[bu])
                            si = cnt % 2
                            act(sg[si], ps[bg][:], ACT.Silu, [PB[bg]], [Bsg[si]])
                            tt("dve", actT[:, f, g * 512:(g + 1) * 512], sg[si], ps[bu][:], ALU.mult,
                               [Bsg[si], PB[bu]], [BaT[f][g]])
                if FSTAGE < 3:
                    P.barrier()
                    continue
                P.add("pool", lambda e: e.dma_start(out=w2hi, in_=w2v[:, 11:22, :]),
                      writes=[Bw2[1]] + BhT + [Bwgu[0]], dma=True)
                def ybank(i):
                    return ((i % 3) * 2, (i % 3) * 2 + 1)

                def down_mm(i, f0, f1):
                    yb = ybank(i)
                    for oh in range(2):
                        for f in range(f0, f1):
                            mm(ps[yb[oh]][:], actT[:, f, i * 128:(i + 1) * 128], w2p[f // 11][:, f % 11, oh * 512:(oh + 1) * 512],
                               f == 0, f == NF - 1, [BaT[f][i // 4], Bw2[f // 11]], PB[yb[oh]])

                for i in range(3):
                    down_mm(i, 0, 11)
                for i in range(8):
                    t = hh * 8 + i
                    yb = ybank(i)
                    if i < 3:
                        down_mm(i, 11, NF)
                    else:
                        down_mm(i, 0, NF)
                    postnorm_add(t, yb, g2, Bg2, True)
                P.barrier()


        def v3(ap2, n):
            return ap2.rearrange("p (a n) -> p a n", n=n)

        def cross(l):
            A.reset()
            cn = W["cross_norms"]
            hb = [A.alloc(D) for _ in range(3)]
            Bhb = [NB("hb0"), NB("hb1"), NB("hb2")]
            g1, g2, g3 = A.alloc(D, F32), A.alloc(D, F32), A.alloc(D, F32)
            Bg1, Bg2, Bg3 = NB("g1"), NB("g2"), NB("g3")
            load_gbc(g1, cn[l, 0, :], Bg1)
            load_gbc(g2, cn[l, 1, :], Bg2)
            load_gbc(g3, cn[l, 2, :], Bg3)
            wq = v3(A.alloc(8 * 1024), 1024)
            wo = v3(A.alloc(8 * 1024), 1024)
            Bwq, Bwo = NB("cwq"), NB("cwo")
            wload(wq, kview(W["cross_wq"][l]), Bwq)
            wkv = [v3(A.alloc(8 * 512), 512) for _ in range(2)]
            Bwkv = [NB("wkv0"), NB("wkv1")]
            memf = A.alloc(D, F32)
            Bmemf = NB("memf")
            memT = v3(A.alloc(8 * 256), 256)
            BmemT = NB("memT")
            kTc = v3(A.alloc(8 * 256), 256)
            BkTc = NB("kTc")
            Vc = v3(A.alloc(2 * 1024), 1024)
            BVc = NB("Vc")
            hTt = [v3(A.alloc(8 * 128), 128) for _ in range(3)]
            BhTt = [NB("hTt0"), NB("hTt1"), NB("hTt2")]
            qTt = [v3(A.alloc(8 * 128), 128) for _ in range(3)]
            BqTt = [NB("qTt0"), NB("qTt1"), NB("qTt2")]
            Pb = [v3(A.alloc(4 * 256), 256) for _ in range(3)]
            BPb = [NB("Pb0"), NB("Pb1"), NB("Pb2")]
            PT = [v3(A.alloc(8 * 128), 128) for _ in range(3)]
            BPT = [NB("PT0"), NB("PT1"), NB("PT2")]
            ob = [A.alloc(D) for _ in range(3)]
            Bob = [NB("ob0"), NB("ob1"), NB("ob2")]
            oT = [v3(A.alloc(8 * 128), 128) for _ in range(3)]
            BoT = [NB("oT0"), NB("oT1"), NB("oT2")]
            pv6 = v3(psb(6), 128)
            pv7 = v3(psb(7), 128)
            for mt in range(2):
                dma("sp", memf, mem_d[mt * 128:(mt + 1) * 128, :], [], [Bmemf])
                so, sbo = stat_slot()
                rstd_of(memf, D, [Bmemf], so[:, 0:1], sbo)
                stt("dve", hb[mt], memf, so[:, 0:1], g3, ALU.mult, ALU.mult, [Bmemf, sbo, Bg3], [Bhb[mt]])
                for k in range(8):
                    tr(pv6[:, k, :], hb[mt][:, k * 128:(k + 1) * 128], [Bhb[mt]], PB[6])
                cp("dve", memT[:, :, mt * 128:(mt + 1) * 128], pv6, [PB[6]], [BmemT])
            wkvv = kview(W["cross_wkv"][l])
            for cg in range(4):
                wb = cg % 2
                wload(wkv[wb], wkvv[:, :, cg * 512:(cg + 1) * 512], Bwkv[wb])
                if cg < 2:
                    for q in range(4):
                        hj = cg * 4 + q
                        bank = q % 2
                        for k in range(8):
                            mm(ps[bank][:, 0:256], wkv[wb][:, k, q * 128:(q + 1) * 128], memT[:, k, :],
                               k == 0, k == 7, [Bwkv[wb], BmemT], PB[bank])
                        cp("act", kTc[:, hj, :], ps[bank][:, 0:256], [PB[bank]], [BkTc])
                else:
                    oh = cg - 2
                    for mt in range(2):
                        bank = 2 + mt
                        for k in range(8):
                            mm(ps[bank][:], memT[:, k, mt * 128:(mt + 1) * 128], wkv[wb][:, k, :],
                               k == 0, k == 7, [Bwkv[wb], BmemT], PB[bank])
                        cp("act", Vc[:, mt, oh * 512:(oh + 1) * 512], ps[bank][:], [PB[bank]], [BVc])
            wload(wo, kview(W["cross_wo"][l]), Bwo)
            def c_tile(t):
                w = t % 2
                b0, b1 = 2 * w, 2 * w + 1
                yield from prenorm_gen(t, g1, Bg1, hb[w], Bhb[w], hTt[w], BhTt[w], pbank=6, cache=False)
                for hj in range(8):
                    bank = b0 + hj // 4
                    o = ps[bank][:, (hj % 4) * 128:(hj % 4 + 1) * 128]
                    for k in range(8):
                        mm(o, wq[:, k, hj * 128:(hj + 1) * 128], hTt[w][:, k, :], k == 0, k == 7,
                           [Bwq, BhTt[w]], PB[bank])
                    if hj % 4 == 3:
                        yield
                for i, bank in enumerate((b0, b1)):
                    cp("act" if i else "dve", qTt[w][:, i * 4:(i + 1) * 4, :], v3(ps[bank][:], 128),
                       [PB[bank]], [BqTt[w]])
                yield
                for h in range(4):
                    bank = b0 + h // 2
                    o = ps[bank][:, (h % 2) * 256:(h % 2 + 1) * 256]
                    for j in range(2):
                        mm(o, qTt[w][:, 2 * h + j, :], kTc[:, 2 * h + j, :], j == 0, j == 1, [BqTt[w], BkTc], PB[bank])
                yield
                sl, sb_ = stat_slot()
                sl2, sb2 = stat_slot()
                for i, bank in enumerate((b0, b1)):
                    P.add("dve", lambda e, bank=bank, sl=sl, i=i: e.reduce_max(
                        out=sl[:, 2 * i:2 * i + 2], in_=v3(ps[bank][:], 256), axis=AX.X),
                        reads=[PB[bank]], writes=[sb_])
                ts("dve", sl, sl, -1.0 / 16, None, ALU.mult, None, [sb_], [sb_])
                yield
                for h in range(4):
                    bank = b0 + h // 2
                    act(Pb[w][:, h, :], ps[bank][:, (h % 2) * 256:(h % 2 + 1) * 256], ACT.Exp, [PB[bank], sb_],
                        [BPb[w], sb2], scale=1.0 / 16, bias=sl[:, h:h + 1], accum=sl2[:, h:h + 1])
                yield
                recip(sl2, sl2, [sb2], [sb2])
                for h in range(4):
                    for m in range(2):
                        tr(pv7[:, 2 * h + m, :], Pb[w][:, h, m * 128:(m + 1) * 128], [BPb[w]], PB[7])
                cp("dve", PT[w], pv7, [PB[7]], [BPT[w]])
                yield
                for h in range(4):
                    bank = b0 + h // 2
                    o = ps[bank][:, (h % 2) * 256:(h % 2 + 1) * 256]
                    for m in range(2):
                        mm(o, PT[w][:, 2 * h + m, :], Vc[:, m, h * 256:(h + 1) * 256], m == 0, m == 1, [BPT[w], BVc], PB[bank])
                yield
                for i, bank in enumerate((b0, b1)):
                    tt("dve", v3(ob[w][:, i * 512:(i + 1) * 512], 256), v3(ps[bank][:], 256),
                       bc_mid(sl2[:, 2 * i:2 * i + 2], 256), ALU.mult, [PB[bank], sb2], [Bob[w]])
                yield
                for k in range(8):
                    tr(pv6[:, k, :], ob[w][:, k * 128:(k + 1) * 128], [Bob[w]], PB[6])
                cp("act", oT[w], pv6, [PB[6]], [BoT[w]])
                yield
                for oh, bank in enumerate((b0, b1)):
                    for k in range(8):
                        mm(ps[bank][:], oT[w][:, k, :], wo[:, k, oh * 512:(oh + 1) * 512], k == 0, k == 7, [BoT[w], Bwo], PB[bank])
                    yield
                postnorm_add(t, (b0, b1), g2, Bg2, False)
                yield

            interleave([c_tile(t) for t in range(NT)], 2)
            P.barrier()

        dbgflag = [False]
        BhTd = [Buf("hTd%d" % t) for t in range(NT)]

        def hT_store(t, src3, Bsrc):
            dma("sp", hTd[t], src3.rearrange("p k n -> p (k n)"), [Bsrc], [BhTd[t]])

        ByTd = [[Buf("yTd%d_%d" % (b, g)) for g in range(4)] for b in range(3)]

        def yT_store(b, k0, nk, t, src3, Bsrc):
            dst = yTd[b].rearrange("p (k n) -> p k n", n=2048)[:, k0:k0 + nk, t * 128:(t + 1) * 128]
            dma("sp", dst, src3, [Bsrc], [ByTd[b][t // 4]])

        def hT_load(t, dst3, Bdst):
            dma("sp", dst3, hTd[t].rearrange("p (k n) -> p k n", n=128), [BhTd[t]], [Bdst])

        def mixer(l, dbg=None):
            A.reset()
            mn = W["mix_norms"]
            g1 = A.alloc(D, F32)
            Bg1 = NB("g1")
            load_gbc(g1, mn[l, 0, :], Bg1)
            hb = [A.alloc(D) for _ in range(2)]
            Bhb = [NB("hb0"), NB("hb1")]
            hTt = [v3(A.alloc(8 * 128), 128) for _ in range(2)]
            BhTt = [NB("hTt0"), NB("hTt1")]
            mark = A.off
            win = kview(W["w_in"][l])
            pv6 = v3(psb(6), 128)
            pv7 = v3(psb(7), 128)

            def hT_tile(t, first):
                i2 = t % 2
                prenorm_tile(t, g1, Bg1, hb[i2], Bhb[i2], hTt[i2], BhTt[i2], pbank=6, cache=not first)
                return hTt[i2], BhTt[i2]

            Wr = v3(A.alloc(8 * 1536), 1536)
            BWr = [NB("Wr0"), NB("Wr1"), NB("Wr2")]
            for i in range(3):
                wload(Wr[:, :, i * 512:(i + 1) * 512], win[:, :, i * 512:(i + 1) * 512], BWr[i])
            St = v3(A.alloc(2 * 128, F32), 128)
            BSt = NB("St")
            Sb = [v3(A.alloc(2 * 128), 128) for _ in range(3)]
            BSb = [NB("Sb%d" % i) for i in range(3)]
            memset("pool", St, 0.0, [BSt])

            def hv(ap3, j):
                return ap3.rearrange("p (c j) n -> p c j n", j=2)[:, :, j, :]

            RS = []
            for w in range(2):
                d_ = {}
                d_["qkf"] = A.alloc(512, F32)
                for nm in ("ta", "tb", "tc", "td"):
                    d_[nm] = A.alloc(256, F32)
                d_["qkr"] = A.alloc(512)
                d_["kdt"] = A.alloc(256)
                d_["vt"] = A.alloc(512)
                d_["kT"] = v3(A.alloc(2 * 128), 128)
                d_["qpad"] = v3(A.alloc(4 * 128), 128)
                d_["qdpad"] = v3(A.alloc(4 * 128), 128)
                d_["AT"] = v3(A.alloc(4 * 128), 128)
                d_["on"] = A.alloc(512, F32)
                d_["sgr"] = A.alloc(512, F32)
                d_["yb"] = A.alloc(512)
                d_["ys"] = v3(A.alloc(4 * 128), 128)
                d_["B"] = {k: NB("r%s%d" % (k, w)) for k in ("qkf", "ta", "tb", "tc", "td", "qkr1", "qkr2", "kdt", "vt",
                                                              "kT", "qpad", "qdpad", "AT", "on", "sgr", "yb", "ys")}
                memset("pool", d_["qpad"], 0.0, [d_["B"]["qpad"]])
                memset("pool", d_["qdpad"], 0.0, [d_["B"]["qdpad"]])
                RS.append(d_)

            def r_tile(t):
                w = t % 2
                R_ = RS[w]
                B_ = R_["B"]
                bA, bB, bC = 3 * w, 3 * w + 1, 3 * w + 2
                pvT = pv7 if w == 0 else pv6
                PBT = PB[7] if w == 0 else PB[6]
                yield from prenorm_gen(t, g1, Bg1, hb[w], Bhb[w], hTt[w], BhTt[w], pbank=6, cache=False)
                hT_, BhT_ = hTt[w], BhTt[w]
                hT_store(t, hT_, BhT_)
                for (bank, c0) in ((bA, 0), (bB, 512), (bC, 1024)):
                    for k in range(8):
                        mm(ps[bank][:], hT_[:, k, :], Wr[:, k, c0:c0 + 512], k == 0, k == 7,
                           [BhT_, BWr[c0 // 512]], PB[bank])
                    yield
                qkf, qkr, kdt, vt = R_["qkf"], R_["qkr"], R_["kdt"], R_["vt"]
                cp("act", qkf, ps[bA][:], [PB[bA]], [B_["qkf"]])
                cp("act", vt, ps[bB][:], [PB[bB]], [B_["vt"]])
                act(R_["sgr"], ps[bC][:], ACT.Silu, [PB[bC]], [B_["sgr"]])
                yield
                q3 = v3(qkf, 64)
                x1, x2 = q3[:, :, 0:32], q3[:, :, 32:64]
                cs, sn = bc_in(cosr[:, t, :], 8), bc_in(sinr[:, t, :], 8)
                o3 = v3(qkr, 64)
                ta, tb, tc_, td = R_["ta"], R_["tb"], R_["tc"], R_["td"]
                tt("dve", v3(ta, 32), x1, cs, ALU.mult, [B_["qkf"], Btab], [B_["ta"]])
                tt("dve", v3(tb, 32), x2, sn, ALU.mult, [B_["qkf"], Btab], [B_["tb"]])
                tt("pool", v3(tc_, 32), x2, cs, ALU.mult, [B_["qkf"], Btab], [B_["tc"]])
                tt("pool", v3(td, 32), x1, sn, ALU.mult, [B_["qkf"], Btab], [B_["td"]])
                yield
                tt("dve", o3[:, :, 0:32], v3(ta, 32), v3(tb, 32), ALU.subtract, [B_["ta"], B_["tb"]], [B_["qkr1"]])
                tt("pool", o3[:, :, 32:64], v3(tc_, 32), v3(td, 32), ALU.add, [B_["tc"], B_["td"]], [B_["qkr2"]])
                yield
                tt("dve", v3(kdt, 64), v3(qkr[:, 256:512], 64), bc_mid(kdec, 64), ALU.mult,
                   [B_["qkr1"], B_["qkr2"], Brc], [B_["kdt"]])
                for c in range(4):
                    tr(pvT[:, c, :], qkr[:, c * 128:(c + 1) * 128], [B_["qkr1"], B_["qkr2"]], PBT)
                cp("act", R_["kT"], pvT[:, 2:4, :], [PBT], [B_["kT"]])
                for h in range(4):
                    j, c = h % 2, h // 2
                    pr = slice(j * 64, (j + 1) * 64)
                    cp("dve" if h % 2 else "act", R_["qpad"][pr, h, :], pvT[pr, c, :], [PBT], [B_["qpad"]])
                    tt("dve", R_["qdpad"][pr, h, :], pvT[pr, c, :], qdecT[pr, c, :], ALU.mult, [PBT, Brc], [B_["qdpad"]])
                yield
                KV = ps[bC][:].rearrange("p (c j n) -> p c j n", c=2, j=2)
                for c in range(2):
                    for j in range(2):
                        mm(KV[:, c, j, :], kdt[:, c * 128:(c + 1) * 128], vt[:, (2 * c + j) * 128:(2 * c + j + 1) * 128],
                           True, True, [B_["kdt"], B_["vt"]], PB[bC])
                for c in range(2):
                    for j in range(2):
                        pr = slice(j * 64, (j + 1) * 64)
                        stt("dve", St[pr, c, :], St[pr, c, :], cdcol[pr, c:c + 1], KV[pr, c, j, :], ALU.mult, ALU.add,
                            [BSt, PB[bC], Brc], [BSt])
                cp("act", Sb[(t + 1) % 3], St, [BSt], [BSb[(t + 1) % 3]])
                yield
                S3 = v3(ps[bA][:], 128)
                for h in range(4):
                    mm(S3[:, h, :], R_["kT"][:, h // 2, :], R_["qpad"][:, h, :], True, True, [B_["kT"], B_["qpad"]], PB[bA])
                yield
                tt("dve", R_["AT"], S3, dmaskT, ALU.mult, [PB[bA], Brc], [B_["AT"]])
                yield
                O3 = v3(ps[bB][:], 128)
                for h in range(4):
                    mm(O3[:, h, :], R_["AT"][:, h, :], vt[:, h * 128:(h + 1) * 128], True, t == 0, [B_["AT"], B_["vt"]], PB[bB])
                    if t > 0:
                        mm(O3[:, h, :], R_["qdpad"][:, h, :], Sb[t % 3][:, h // 2, :], False, True,
                           [B_["qdpad"], BSb[t % 3]], PB[bB])
                yield
                sl, sb_ = stat_slot()
                for h in range(4):
                    act(junk[:, 0:128], O3[:, h, :], ACT.Square, [PB[bB]], [sb_], accum=sl[:, h:h + 1])
                so, sbo = stat_slot()
                act(sl, sl, ACT.Ln, [sb_, Beps], [sb_], scale=1.0 / 128, bias=epsT[:, 0:1])
                act(so, sl, ACT.Exp, [sb_], [sbo], scale=-0.5)
                yield
                tt("dve", v3(R_["on"], 128), O3, bc_mid(so, 128), ALU.mult, [PB[bB], sbo], [B_["on"]])
                yield
                tt("pool", R_["yb"], R_["on"], R_["sgr"], ALU.mult, [B_["on"], B_["sgr"]], [B_["yb"]])
                yield
                for c in range(4):
                    tr(pvT[:, 4 + c, :], R_["yb"][:, c * 128:(c + 1) * 128], [B_["yb"]], PBT)
                cp("act", R_["ys"], pvT[:, 4:8, :], [PBT], [B_["ys"]])
                yT_store(0, 0, 4, t, R_["ys"], B_["ys"])
                if dbg == "ret":
                    dbgflag[0] = True
                    dma("pool", out_d[t * 128:(t + 1) * 128, 0:512], R_["yb"], [B_["yb"]], [NB("dbgout")])
                yield

            interleave([r_tile(t) for t in range(NT)], 2)
            P.barrier()
            A.off = mark
            if dbg == "ret":
                return

            lam_i = 0.8 - 0.6 * math.exp(-0.3 * l)
            dlb = A.alloc(256, F32)
            Bdlb = NB("dlb")
            load_gbc(dlb, W["diff_lambda"][l, :], Bdlb)
            lsl, Blsl = A.alloc(8, F32), NB("lsl")
            dtmp = A.alloc(128, F32)
            Bdtmp = NB("dtmp")
            tt("dve", v3(dtmp, 64), v3(dlb, 64)[:, 0:4:2, :], v3(dlb, 64)[:, 1:4:2, :], ALU.mult, [Bdlb], [Bdtmp])
            P.add("dve", lambda e: e.reduce_sum(out=lsl[:, 0:2], in_=v3(dtmp, 64), axis=AX.X), reads=[Bdtmp], writes=[Blsl])
            act(lsl[:, 2:4], lsl[:, 0:2], ACT.Exp, [Blsl], [Blsl])
            tt("dve", lsl[:, 4:5], lsl[:, 3:4], lsl[:, 2:3], ALU.subtract, [Blsl], [Blsl])
            ts("dve", lsl[:, 5:6], lsl[:, 4:5], -lam_i, None, ALU.add, None, [Blsl], [Blsl])
            neglam = lsl[:, 5:6]
            if True:
                hp = 0
                Wd = v3(A.alloc(8 * 1536), 1536)
                BWd = NB("Wd")
                P.add("pool", lambda e: [
                    e.dma_start(out=Wd[:, :, i * 512:(i + 1) * 512], in_=win[:, :, 1952 + i * 512:1952 + (i + 1) * 512])
                    for i in range(3)], writes=[BWd], dma=True, ndma=3)
                qpad = A.alloc(4 * 2 * 2048).rearrange("p (h j n) -> p h j n", h=4, j=2)
                kTd = v3(A.alloc(4 * 2048), 2048)
                Vaug = A.alloc(NT * 4 * 130).rearrange("p (t h n) -> p t h n", t=NT, h=4)
                Bq = [NB("dq%d" % t) for t in range(NT)]
                Bk = [NB("dk%d" % t) for t in range(NT)]
                Bv = [NB("dv%d" % t) for t in range(NT)]
                Bones = NB("dones")
                qr, kr = [A.alloc(512) for _ in range(2)], [A.alloc(512) for _ in range(2)]
                Bqr, Bkr = [NB("dqr0"), NB("dqr1")], [NB("dkr0"), NB("dkr1")]
                ra, rb = [A.alloc(64, F32) for _ in range(2)], [A.alloc(64, F32) for _ in range(2)]
                Bra, Brb = [NB("dra0"), NB("dra1")], [NB("drb0"), NB("drb1")]
                rc, rd = [A.alloc(64, F32) for _ in range(2)], [A.alloc(64, F32) for _ in range(2)]
                Brc_, Brd = [NB("drc0"), NB("drc1")], [NB("drd0"), NB("drd1")]
                Pt = [A.alloc(512) for _ in range(4)]
                BPt = [NB("dP%d" % i) for i in range(4)]
                o1, o2 = [A.alloc(128, F32) for _ in range(4)], [A.alloc(128, F32) for _ in range(4)]
                Bo1, Bo2 = [NB("do1%d" % i) for i in range(4)], [NB("do2%d" % i) for i in range(4)]
                yh = [A.alloc(128) for _ in range(4)]
                Byh = [NB("dyh%d" % i) for i in range(4)]
                ysd = [A.alloc(128) for _ in range(4)]
                Bysd = [NB("dys%d" % i) for i in range(4)]
                memset("pool", qpad, 0.0, Bq)
                memset("pool", Vaug[:, :, :, 128:130], 1.0, [Bones])
                def d_proj(t):
                    w = t % 2
                    hT_load(t, hTt[w], BhTt[w])
                    yield
                    hT_, BhT_ = hTt[w], BhTt[w]
                    bk = (0, 1, 2) if w == 0 else (3, 4, 5)
                    for i in range(3):
                        for k in range(8):
                            mm(ps[bk[i]][:], hT_[:, k, :], Wd[:, k, i * 512:(i + 1) * 512], k == 0, k == 7,
                               [BhT_, BWd], PB[bk[i]])
                        yield
                    cs, sn = bc_in(cosd[:, t, :], 8), bc_in(sind[:, t, :], 8)
                    for (bank, dst, Bd) in ((bk[0], qr[w], Bqr[w]), (bk[1], kr[w], Bkr[w])):
                        cp("act", dst, ps[bank][:], [PB[bank]], [Bd])
                        s3 = v3(ps[bank][:], 64)
                        d3 = v3(dst, 64)
                        x1, x2 = s3[:, :, 0:8], s3[:, :, 8:16]
                        tt("dve", v3(ra[w], 8), x1, cs, ALU.mult, [PB[bank], Btab], [Bra[w]])
                        tt("dve", v3(rb[w], 8), x2, sn, ALU.mult, [PB[bank], Btab], [Brb[w]])
                        yield
                        tt("dve", v3(rc[w], 8), x2, cs, ALU.mult, [PB[bank], Btab], [Brc_[w]])
                        tt("dve", v3(rd[w], 8), x1, sn, ALU.mult, [PB[bank], Btab], [Brd[w]])
                        tt("dve", d3[:, :, 0:8], v3(ra[w], 8), v3(rb[w], 8), ALU.subtract, [Bra[w], Brb[w]], [Bd])
                        yield
                        tt("dve", d3[:, :, 8:16], v3(rc[w], 8), v3(rd[w], 8), ALU.add, [Brc_[w], Brd[w]], [Bd])
                    cp("act", Vaug[:, t, :, 0:128], v3(ps[bk[2]][:], 128), [PB[bk[2]]], [Bv[t]])
                    yield
                    pvw, PBw = (pv7, PB[7]) if w == 0 else (pv6, PB[6])
                    for c in range(4):
                        tr(pvw[:, c, :], qr[w][:, c * 128:(c + 1) * 128], [Bqr[w]], PBw)
                        tr(pvw[:, 4 + c, :], kr[w][:, c * 128:(c + 1) * 128], [Bkr[w]], PBw)
                    yield
                    ts_ = slice(t * 128, (t + 1) * 128)
                    cp("dve", qpad[0:64, :, 0, ts_], pvw[0:64, 0:4, :], [PBw], [Bq[t]])
                    cp("act", qpad[64:128, :, 1, ts_], pvw[64:128, 0:4, :], [PBw], [Bq[t]])
                    cp("dve", kTd[:, :, ts_], pvw[:, 4:8, :], [PBw], [Bk[t]])
                    yield

                interleave([d_proj(t) for t in range(NT)], 2)
                its = []
                for G2 in range(8):
                    q0, q1 = 2 * G2, 2 * G2 + 1
                    for hl in range(4):
                        for kt in range(q1 + 1):
                            its.append((q0, q1, hl, kt, len(its)))

                SBK = [0, 1, 6]
                SAP = [ps[0][:], ps[1][:], ps[6][:].bitcast(F32)]
                LA = 2

                def d_S(it):
                    q0, q1, hl, kt, n = it
                    qs = max(kt, q0)
                    N = (q1 + 1 - qs) * 128
                    sbk = SBK[n % 3]
                    S = v3(SAP[n % 3], 256)
                    for j in range(2):
                        mm(S[:, j, 0:N], kTd[:, hl, kt * 128:(kt + 1) * 128], qpad[:, hl, j, qs * 128:(q1 + 1) * 128],
                           True, True, [Bk[kt]] + Bq[qs:q1 + 1], PB[sbk])

                deferred = []

                def d_rest(it):
                    q0, q1, hl, kt, n = it
                    h = 2 * hp + hl
                    qs = max(kt, q0)
                    N = (q1 + 1 - qs) * 128
                    sbk = SBK[n % 3]
                    pi = n % 4
                    S = v3(SAP[n % 3], 256)
                    Pv = v3(Pt[pi], 256)
                    act(Pv[:, :, 0:N], S[:, :, 0:N], ACT.Exp, [PB[sbk]], [BPt[pi]], scale=0.125)
                    if kt == qs:
                        memset("pool", Pv[64:128, :, 0:64], 0.0, [BPt[pi]])
                    for qi, qt in enumerate(range(qs, q1 + 1)):
                        obs = [2 + (qt % 2) * 2 + j for j in range(2)]
                        Oj2 = [ps[b][:, 0:129] for b in obs]
                        for j in range(2):
                            mm(Oj2[j], Pv[:, j, qi * 128:(qi + 1) * 128], Vaug[:, kt, hl, 0:129],
                               kt == 0, kt == qt, [BPt[pi], Bv[kt], Bones], PB[obs[j]])
                        if kt == qt:
                            fi = fin_i[0] % 4
                            fin_i[0] += 1
                            sl, sb_ = stat_slot()
                            for j in range(2):
                                recip(sl[:, j:j + 1], Oj2[j][:, 128:129], [PB[obs[j]]], [sb_])
                            tt("dve", sl[:, 2:3], sl[:, 1:2], neglam, ALU.mult, [sb_, Blsl], [sb_])
                            ts("dve", o1[fi], Oj2[1][:, 0:128], sl[:, 2:3], None, ALU.mult, None, [PB[obs[1]], sb_], [Bo1[fi]])
                            stt("dve", o2[fi], Oj2[0][:, 0:128], sl[:, 0:1], o1[fi], ALU.mult, ALU.add,
                                [PB[obs[0]], sb_, Bo1[fi]], [Bo2[fi]])

                            def finB(fi=fi):
                                so, sbo = stat_slot()
                                act(junk[:, 0:128], o2[fi], ACT.Square, [Bo2[fi]], [sbo], accum=so[:, 0:1])
                                act(so[:, 1:2], so[:, 0:1], ACT.Ln, [sbo, Beps], [sbo], scale=1.0 / 128, bias=epsT[:, 0:1])
                                act(so[:, 2:3], so[:, 1:2], ACT.Exp, [sbo], [sbo], scale=-0.5)
                                ts("dve", yh[fi], o2[fi], so[:, 2:3], 1.0 - lam_i, ALU.mult, ALU.mult, [Bo2[fi], sbo], [Byh[fi]])

                            def finC(fi=fi, h=h, qt=qt):
                                tr(pv7[:, 4 + fi, :], yh[fi], [Byh[fi]], PB[7])
                                cp("act", ysd[fi], pv7[:, 4 + fi, :], [PB[7]], [Bysd[fi]])
                                yT_store(2, h, 1, qt, ysd[fi].unsqueeze(1), Bysd[fi])
                                if dbg == "diff":
                                    dbgflag[0] = True
                                    dma("pool", out_d[qt * 128:(qt + 1) * 128, h * 128:(h + 1) * 128], yh[fi], [Byh[fi]],
                                        [NB("dbgout")])
                            deferred.append((n + 2, finB))
                            deferredC.append((n + 4, finC))

                deferredC = []
                fin_i = [0]
                for n0 in range(LA):
                    d_S(its[n0])
                for n, it in enumerate(its):
                    while deferredC and deferredC[0][0] <= n:
                        deferredC.pop(0)[1]()
                    while deferred and deferred[0][0] <= n:
                        deferred.pop(0)[1]()
                    if n + LA < len(its):
                        d_S(its[n + LA])
                    d_rest(it)
                while deferred:
                    deferred.pop(0)[1]()
                while deferredC:
                    deferredC.pop(0)[1]()
                P.barrier()
            A.off = mark
            if dbg == "diff":
                return

            qn, kn = A.alloc(256, F32), A.alloc(128, F32)
            Bqn, Bkn = NB("qn"), NB("kn")
            load_gbc(qn, W["mla_q_norm"][l, :], Bqn)
            load_gbc(kn, W["mla_kv_norm"][l, :], Bkn)
            cT = v3(A.alloc(3 * 2048), 2048)
            BcT = [NB("cT%d" % t) for t in range(NT)]
            krA = v3(A.alloc(NT * 32), 32)
            Bkra = [NB("kra%d" % t) for t in range(NT)]
            mark3 = A.off
            Wa = v3(A.alloc(8 * 416), 416)
            BWa = NB("Wa")
            wload(Wa, win[:, :, 1536:1952], BWa)
            cqn, ckn = [A.alloc(256) for _ in range(2)], [A.alloc(128) for _ in range(2)]
            Bcqn, Bckn = [NB("cqn0"), NB("cqn1")], [NB("ckn0"), NB("ckn1")]
            ra, rb = [A.alloc(64, F32) for _ in range(2)], [A.alloc(64, F32) for _ in range(2)]
            Bra, Brb = [NB("mra0"), NB("mra1")], [NB("mrb0"), NB("mrb1")]
            Bra2, Brb2 = [NB("mra20"), NB("mra21")], [NB("mrb20"), NB("mrb21")]

            def l_prep(t):
                w = t % 2
                hT_load(t, hTt[w], BhTt[w])
                yield
                hT_, BhT_ = hTt[w], BhTt[w]
                pa = ps[w]
                for k in range(8):
                    mm(pa[:, 0:416], hT_[:, k, :], Wa[:, k, :], k == 0, k == 7, [BhT_, BWa], PB[w])
                yield
                so, sbo = stat_slot()
                rstd_of(pa[:, 0:256], 256, [PB[w]], so[:, 0:1], sbo)
                so2, sbo2 = stat_slot()
                rstd_of(pa[:, 256:384], 128, [PB[w]], so2[:, 0:1], sbo2)
                yield
                x1, x2 = pa[:, 384:400], pa[:, 400:416]
                cs, sn = cosm[:, t, :], sinm[:, t, :]
                tt("dve", ra[w][:, 0:16], x1, cs, ALU.mult, [PB[w], Btab], [Bra[w]])
                tt("dve", rb[w][:, 0:16], x2, sn, ALU.mult, [PB[w], Btab], [Brb[w]])
                yield
                tt("dve", ra[w][:, 16:32], x2, cs, ALU.mult, [PB[w], Btab], [Bra2[w]])
                tt("dve", rb[w][:, 16:32], x1, sn, ALU.mult, [PB[w], Btab], [Brb2[w]])
                tt("dve", krA[:, t, 0:16], ra[w][:, 0:16], rb[w][:, 0:16], ALU.subtract, [Bra[w], Brb[w]], [Bkra[t]])
                yield
                tt("dve", krA[:, t, 16:32], ra[w][:, 16:32], rb[w][:, 16:32], ALU.add, [Bra2[w], Brb2[w]], [Bkra[t]])
                stt("dve", cqn[w], pa[:, 0:256], so[:, 0:1], qn, ALU.mult, ALU.mult, [PB[w], sbo, Bqn], [Bcqn[w]])
                stt("dve", ckn[w], pa[:, 256:384], so2[:, 0:1], kn, ALU.mult, ALU.mult, [PB[w], sbo2, Bkn], [Bckn[w]])
                yield
                c0 = 3 * w
                tr(pv7[:, c0 + 0, :], cqn[w][:, 0:128], [Bcqn[w]], PB[7])
                tr(pv7[:, c0 + 1, :], cqn[w][:, 128:256], [Bcqn[w]], PB[7])
                tr(pv7[:, c0 + 2, :], ckn[w], [Bckn[w]], PB[7])
                yield
                cp("act", cT[:, :, t * 128:(t + 1) * 128], pv7[:, c0:c0 + 3, :], [PB[7]], [BcT[t]])
                yield

            interleave([l_prep(t) for t in range(NT)], 2)
            P.barrier()
            sc_m = 96.0 ** -0.5
            for hg in range(2):
                A.off = mark3
                wqb = v3(A.alloc(2 * 384), 384)
                wkvb = A.alloc(512)
                Bwqb, Bwkvb = NB("wqb"), NB("wkvb")
                wload(wqb, kview(W["mla_wq_b"][l])[:, :, hg * 384:(hg + 1) * 384], Bwqb)
                wload(wkvb, W["mla_wkv_b"][l][:, hg * 512:(hg + 1) * 512], Bwkvb)
                qTm = v3(A.alloc(4 * 2048), 2048)
                kTm = v3(A.alloc(4 * 2048), 2048)
                Vm = A.alloc(NT * 4 * 66).rearrange("p (t h n) -> p t h n", t=NT, h=4)
                Bq = [NB("mq%d" % t) for t in range(NT)]
                Bk = [NB("mk%d" % t) for t in range(NT)]
                Bv = [NB("mv%d" % t) for t in range(NT)]
                Bones = NB("mones")
                Qtm, Ktm = [A.alloc(384) for _ in range(2)], [A.alloc(384) for _ in range(2)]
                BQtm = [NB("Qtm0"), NB("Qtm1")]
                BKtm1, BKtm2 = [NB("Ktm1a"), NB("Ktm1b")], [NB("Ktm2a"), NB("Ktm2b")]
                ra, rb = [A.alloc(64, F32) for _ in range(2)], [A.alloc(64, F32) for _ in range(2)]
                rc, rd = [A.alloc(64, F32) for _ in range(2)], [A.alloc(64, F32) for _ in range(2)]
                Brc_, Brd = [NB("mrc0"), NB("mrc1")], [NB("mrd0"), NB("mrd1")]
                Pt = [A.alloc(256) for _ in range(3)]
                BPt = [NB("mP%d" % i) for i in range(3)]
                ytm = [A.alloc(256) for _ in range(4)]
                Bytm = [NB("ytm%d" % i) for i in range(4)]
                ysm = [v3(A.alloc(2 * 128), 128) for _ in range(4)]
                Bysm = [NB("ysm%d" % i) for i in range(4)]
                memset("pool", Vm[:, :, :, 64:66], 1.0, [Bones])
                def l_proj(t):
                    w = t % 2
                    ts_ = slice(t * 128, (t + 1) * 128)
                    bq, bkv = (0, 1) if w == 0 else (2, 3)
                    pvw = pv7 if w == 0 else pv6
                    PBw = PB[7] if w == 0 else PB[6]
                    for kc in range(2):
                        mm(ps[bq][:, 0:384], cT[:, kc, ts_], wqb[:, kc, :], kc == 0, kc == 1, [BcT[t], Bwqb], PB[bq])
                    mm(ps[bkv][:], cT[:, 2, ts_], wkvb, True, True, [BcT[t], Bwkvb], PB[bkv])
                    yield
                    cp("act", Qtm[w], ps[bq][:, 0:384], [PB[bq]], [BQtm[w]])
                    s3, d3 = v3(ps[bq][:, 0:384], 96), v3(Qtm[w], 96)
                    x1, x2 = s3[:, :, 64:80], s3[:, :, 80:96]
                    cs, sn = bc_in(cosm[:, t, :], 4), bc_in(sinm[:, t, :], 4)
                    tt("dve", v3(ra[w], 16), x1, cs, ALU.mult, [PB[bq], Btab], [Bra[w]])
                    tt("dve", v3(rb[w], 16), x2, sn, ALU.mult, [PB[bq], Btab], [Brb[w]])
                    yield
                    tt("dve", v3(rc[w], 16), x2, cs, ALU.mult, [PB[bq], Btab], [Brc_[w]])
                    tt("dve", v3(rd[w], 16), x1, sn, ALU.mult, [PB[bq], Btab], [Brd[w]])
                    tt("dve", d3[:, :, 64:80], v3(ra[w], 16), v3(rb[w], 16), ALU.subtract, [Bra[w], Brb[w]], [BQtm[w]])
                    yield
                    tt("dve", d3[:, :, 80:96], v3(rc[w], 16), v3(rd[w], 16), ALU.add, [Brc_[w], Brd[w]], [BQtm[w]])
                    kv3 = v3(ps[bkv][:], 128)
                    k3 = v3(Ktm[w], 96)
                    cp("act", k3[:, :, 0:64], kv3[:, :, 0:64], [PB[bkv]], [BKtm1[w]])
                    cp("pool", k3[:, :, 64:96], bc_in(krA[:, t, :], 4), [Bkra[t]], [BKtm2[w]])
                    cp("dve", Vm[:, t, :, 0:64], kv3[:, :, 64:128], [PB[bkv]], [Bv[t]])
                    yield
                    for hl in range(4):
                        tr(pvw[0:96, hl, :], d3[:, hl, :], [BQtm[w]], PBw)
                        tr(pvw[0:96, 4 + hl, :], k3[:, hl, :], [BKtm1[w], BKtm2[w]], PBw)
                    cp("dve", qTm[0:96, :, ts_], pvw[0:96, 0:4, :], [PBw], [Bq[t]])
                    cp("act", kTm[0:96, :, ts_], pvw[0:96, 4:8, :], [PBw], [Bk[t]])
                    yield

                interleave([l_proj(t) for t in range(NT)], 2)
                its = []
                for G2 in range(8):
                    q0, q1 = 2 * G2, 2 * G2 + 1
                    for hl in range(4):
                        for kt in range(q1 + 1):
                            its.append((q0, q1, hl, kt, len(its)))

                SBK = [0, 1, 7]
                SAP = [ps[0][:], ps[1][:], ps[7][:].bitcast(F32)]
                LA = 2

                def m_S(it):
                    q0, q1, hl, kt, n = it
                    qs = max(kt, q0)
                    N = (q1 + 1 - qs) * 128
                    sbk = SBK[n % 3]
                    mm(SAP[n % 3][:, 0:N], kTm[0:96, hl, kt * 128:(kt + 1) * 128], qTm[0:96, hl, qs * 128:(q1 + 1) * 128],
                       True, True, [Bk[kt]] + Bq[qs:q1 + 1], PB[sbk])

                deferred = []

                def m_rest(it):
                    q0, q1, hl, kt, n = it
                    qs = max(kt, q0)
                    N = (q1 + 1 - qs) * 128
                    sbk = SBK[n % 3]
                    pi = n % 3
                    act(Pt[pi][:, 0:N], SAP[n % 3][:, 0:N], ACT.Exp, [PB[sbk]], [BPt[pi]], scale=sc_m)
                    if kt == qs:
                        memset("pool", Pt[pi][64:128, 0:64], 0.0, [BPt[pi]])
                    for qi, qt in enumerate(range(qs, q1 + 1)):
                        ob = 2 + (qt % 2) + 2 * (hl % 2)
                        O = ps[ob][:, 0:65]
                        mm(O, Pt[pi][:, qi * 128:(qi + 1) * 128], Vm[:, kt, hl, 0:65], kt == 0, kt == qt,
                           [BPt[pi], Bv[kt], Bones], PB[ob])
                        if kt == qt:
                            G2 = q0 // 2
                            yi = 2 * (G2 % 2) + (qt % 2)
                            sl, sb_ = stat_slot()
                            recip(sl[:, 0:1], O[:, 64:65], [PB[ob]], [sb_])
                            ts("dve", ytm[yi][:, hl * 64:(hl + 1) * 64], O[:, 0:64], sl[:, 0:1], None, ALU.mult, None,
                               [PB[ob], sb_], [Bytm[yi]])
                            if hl == 3:
                                def fin(qt=qt, yi=yi):
                                    y2 = ytm[yi]
                                    for c in range(2):
                                        tr(pv6[:, 2 * (qt % 2) + c, :], y2[:, c * 128:(c + 1) * 128], [Bytm[yi]], PB[6])
                                    cp("dve", ysm[yi], pv6[:, 2 * (qt % 2):2 * (qt % 2) + 2, :], [PB[6]], [Bysm[yi]])
                                    yT_store(1, 2 * hg, 2, qt, ysm[yi], Bysm[yi])
                                    if dbg == "mla":
                                        dbgflag[0] = True
                                        dma("pool", out_d[qt * 128:(qt + 1) * 128, hg * 256:(hg + 1) * 256], y2, [Bytm[yi]],
                                            [NB("dbgout")])
                                deferred.append((n + 3, fin))

                for n0 in range(LA):
                    m_S(its[n0])
                for n, it in enumerate(its):
                    while deferred and deferred[0][0] <= n:
                        deferred.pop(0)[1]()
                    if n + LA < len(its):
                        m_S(its[n + LA])
                    m_rest(it)
                while deferred:
                    deferred.pop(0)[1]()
                P.barrier()
            A.off = mark
            if dbg == "mla":
                return

            g2 = A.alloc(D, F32)
            Bg2 = NB("g2")
            load_gbc(g2, mn[l, 1, :], Bg2)
            wout = v3(A.alloc(8 * 1024), 1024)
            Bwout = NB("wout")
            wload(wout, kview(W["w_out"][l]), Bwout)
            hTg = v3(A.alloc(8 * 512), 512)
            BhTg = [NB("hTg%d" % i) for i in range(4)]
            yTg = v3(A.alloc(12 * 512), 512)
            ByTg = [NB("yTg%d" % b) for b in range(3)]
            mfc = [A.alloc(512, F32) for _ in range(2)]
            Bmfc = [NB("mfc0"), NB("mfc1")]
            mb = v3(A.alloc(8 * 512), 512)
            Bmb = [NB("mb%d" % c) for c in range(8)]
            wg = [A.alloc(3 * 8 * 128).rearrange("p (b k n) -> p b k n", b=3, k=8) for _ in range(2)]
            wbr = [A.alloc(3 * 4 * 128).rearrange("p (b k n) -> p b k n", b=3, k=4) for _ in range(2)]
            Bwg = [NB("wg0"), NB("wg1")]
            sig = [A.alloc(512, F32) for _ in range(2)]
            Bsig = [NB("sig0"), NB("sig1")]
            tmpm = A.alloc(512, F32)
            Btmpm = NB("tmpm")
            cw = 0
            cs_ = 0
            for g in range(4):
                gs = slice(g * 512, (g + 1) * 512)
                for i in range(4):
                    t = 4 * g + i
                    hT_load(t, hTg[:, :, i * 128:(i + 1) * 128], BhTg[i])
                for b in range(3):
                    dma("sp", yTg[:, 4 * b:4 * b + 4, :], yTd[b].rearrange("p (k n) -> p k n", n=2048)[:, :, gs],
                        [ByTd[b][g]], [ByTg[b]])
                for c in range(8):
                    wi = cw % 2
                    cw += 1
                    P.add("pool", lambda e, wi=wi, c=c: [
                        e.dma_start(out=wg[wi].rearrange("p b k n -> p (b k n)"), in_=W["wgate"][l, c]),
                        e.dma_start(out=wbr[wi].rearrange("p b k n -> p (b k n)"), in_=W["wbr2"][l, c])],
                        writes=[Bwg[wi]], dma=True, ndma=2)
                    mi = c % 2
                    for b in range(3):
                        bg = 2 * b
                        for k in range(8):
                            mm(ps[bg][:], wg[wi][:, b, k, :], hTg[:, k, :], k == 0, k == 7, [Bwg[wi]] + BhTg, PB[bg])
                        for k in range(4):
                            mm(ps[bg + 1][:], wbr[wi][:, b, k, :], yTg[:, 4 * b + k, :], k == 0, k == 3,
                               [Bwg[wi], ByTg[b]], PB[bg + 1])
                        si = cs_ % 2
                        cs_ += 1
                        act(sig[si], ps[bg][:], ACT.Sigmoid, [PB[bg]], [Bsig[si]])
                        if b == 0:
                            tt("dve", mfc[mi], sig[si], ps[bg + 1][:], ALU.mult, [Bsig[si], PB[bg + 1]], [Bmfc[mi]])
                        else:
                            tt("dve", tmpm, sig[si], ps[bg + 1][:], ALU.mult, [Bsig[si], PB[bg + 1]], [Btmpm])
                            tt("dve", mfc[mi], mfc[mi], tmpm, ALU.add, [Bmfc[mi], Btmpm], [Bmfc[mi]])
                    cp("act", mb[:, c, :], mfc[mi], [Bmfc[mi]], [Bmb[c]])
                for i in range(4):
                    t = 4 * g + i
                    yb_ = (0, 1) if i % 2 == 0 else (2, 3)
                    for oh in range(2):
                        for c in range(8):
                            mm(ps[yb_[oh]][:], mb[:, c, i * 128:(i + 1) * 128], wout[:, c, oh * 512:(oh + 1) * 512],
                               c == 0, c == 7, [Bmb[c], Bwout], PB[yb_[oh]])
                    postnorm_add(t, yb_, g2, Bg2, False)
            P.barrier()

        if phases is None:
            phases_ = []
            for l in range(NL):
                phases_ += [(l, "ffn1"), (l, "mix"), (l, "cross"), (l, "ffn2")]
        else:
            phases_ = phases
        for (l, ph) in phases_:
            if ph == "ffn1":
                ffn(l, W["ffn1_w13"], W["ffn1_w2"], W["ffn1_norms"])
            elif ph == "ffn2":
                ffn(l, W["ffn2_w13"], W["ffn2_w2"], W["ffn2_norms"])
            elif ph == "cross":
                cross(l)
            elif ph.startswith("mix"):
                mixer(l, dbg=ph[4:] if len(ph) > 3 else None)

        Bout = NB("dbgout") if dbgflag[0] else Buf("out")
        for t in ([] if dbgflag[0] else range(NT)):
            P.add("sp", lambda e, t=t: e.dma_start(out=out_d[t * 128:(t + 1) * 128, :], in_=X[:, t, :]),
                  reads=[XB[t]], writes=[Bout], dma=True)
        P.add("sp", lambda e: e.nop(), reads=[Bout], real=False)
        P.emit()
    return nc


_CACHE = {}


def make_in_maps(inputs, cores):
    inv, rc = host_consts()
    maps = []
    shared = {}
    for k, v in inputs.items():
        if k in ("x", "mem", "positions"):
            continue
        a = np.ascontiguousarray(v)
        if k == "diff_lambda":
            a = a.reshape(4, 256)
        if k == "w_in":
            gcols = a[:, :, 3488:].reshape(4, 8, 128, 3, 8, 128)
            shared["wgate"] = np.ascontiguousarray(gcols.transpose(0, 4, 2, 3, 1, 5)).reshape(4, 8, 128, 3072)
            a = np.ascontiguousarray(a[:, :, 0:3488])
        if k == "w_branch":
            wb = a.reshape(4, 3, 4, 128, 8, 128)
            shared["wbr2"] = np.ascontiguousarray(wb.transpose(0, 4, 3, 1, 2, 5)).reshape(4, 8, 128, 1536)
            continue
        shared[k] = a
    for b in cores:
        m = dict(shared)
        m["x"] = np.ascontiguousarray(inputs["x"][b])
        m["mem"] = np.ascontiguousarray(inputs["mem"][b])
        m["pos"] = np.ascontiguousarray(inputs["positions"][b].reshape(NT, 128).T.astype(np.int32))
        m["inv"] = inv
        m["rc"] = rc
        maps.append(m)
    return maps


def kernel(**inputs):
    if "nc" not in _CACHE:
        _CACHE["nc"] = build_program()
    nc = _CACHE["nc"]
    maps = make_in_maps(inputs, list(range(8)))
    res = run_bass_kernel_spmd(nc, maps, core_ids=list(range(8)))
    out = np.stack([np.asarray(r["out"]) for r in res.results], axis=0)
    return out.astype(np.float32)
```
